# Optimizing a Trainium2 kernel written in Bass

```python
import jax, jax.numpy as jnp
from jax import lax
import numpy as np

D_MODEL = 1024
BATCH = 4
SEQ = 8192
DEPTH = 2

PLE_DIM = 256
D_MIX = D_MODEL
HEAD_DIM = 64
RMS_EPS = 1e-6

CONV_W = D_MIX // 4
CONV_K = 3

NSA_HEADS = 6
NSA_KV_HEADS = 2
NSA_GROUP = NSA_HEADS // NSA_KV_HEADS
NSA_W = NSA_HEADS * HEAD_DIM
NSA_KV_W = NSA_KV_HEADS * HEAD_DIM
CMP_BLOCK = 32
CMP_STRIDE = 16
SLC_BLOCK = 64
SLC_TOPK = 16
WIN = 512
Q_BLOCK = 128
FORCE_SCORE = 1e4

DIL_PAIRS = ((128, 1), (512, 4), (2048, 16))
DIL_HEADS_PER_PAIR = 2
DIL_HEADS = DIL_HEADS_PER_PAIR * len(DIL_PAIRS)
DIL_W = DIL_HEADS * HEAD_DIM
DIL_BLOCK = 128

IN_SIZES = ((CONV_W,) * 4
            + (NSA_W,) + (NSA_KV_W,) * 6
            + (NSA_HEADS * 3, NSA_W)
            + (DIL_W,) * 4)
D_IN = sum(IN_SIZES)

kernel_name = "hymba_conv_nsa_dilated_hybrid"


def _rmsnorm(x, g):
    x32 = x.astype(jnp.float32)
    y = x32 * lax.rsqrt(jnp.mean(x32 * x32, axis=-1, keepdims=True) + RMS_EPS)
    return (y * g.astype(jnp.float32)).astype(x.dtype)


def _masked_softmax(s, mask):
    s = jnp.where(mask, s.astype(jnp.float32), -jnp.inf)
    m = jnp.max(s, axis=-1, keepdims=True)
    m = jnp.where(jnp.isfinite(m), m, 0.0)
    e = jnp.where(mask, jnp.exp(s - m), 0.0)
    den = jnp.sum(e, axis=-1, keepdims=True)
    return e / jnp.maximum(den, 1e-30), m, den


def _short_conv(u, w, b):
    y = lax.conv_general_dilated(
        u, w[:, None, :].astype(u.dtype), window_strides=(1,), padding=((CONV_K - 1, 0),),
        dimension_numbers=("NWC", "WIO", "NWC"), feature_group_count=u.shape[-1])
    return y + b.astype(u.dtype)


def _nsa(q, kc_raw, vc_raw, ks_raw, vs_raw, kw_raw, vw_raw, gates, pe, w1, w2):
    B, T, G, R, E = q.shape
    scale = E ** -0.5
    n_cmp = (T - CMP_BLOCK) // CMP_STRIDE + 1
    cmp_start = jnp.arange(n_cmp) * CMP_STRIDE
    cmp_idx = cmp_start[:, None] + jnp.arange(CMP_BLOCK)[None, :]

    def compress(a, pe_, w1_, w2_):
        blk = a[:, cmp_idx] + pe_[None, None, :, None, :].astype(a.dtype)
        flat = jnp.moveaxis(blk, 3, 2).reshape(B, n_cmp, G, CMP_BLOCK * E)
        return jax.nn.silu(flat @ w1_) @ w2_

    k_cmp = compress(kc_raw, pe[0], w1[0], w2[0])
    v_cmp = compress(vc_raw, pe[1], w1[1], w2[1])
    cmp_last = cmp_start + CMP_BLOCK - 1
    n_slc = T // SLC_BLOCK
    slc_start = jnp.arange(n_slc) * SLC_BLOCK
    overlap = ((cmp_start[:, None] < slc_start[None, :] + SLC_BLOCK)
               & (cmp_start[:, None] + CMP_BLOCK > slc_start[None, :])).astype(jnp.float32)
    k_slc = ks_raw.reshape(B, n_slc, SLC_BLOCK, G, E).transpose(0, 3, 1, 2, 4)
    v_slc = vs_raw.reshape(B, n_slc, SLC_BLOCK, G, E).transpose(0, 3, 1, 2, 4)
    top = min(SLC_TOPK, n_slc)
    b_ix = jnp.arange(B)[:, None, None, None]
    g_ix = jnp.arange(G)[None, None, :, None]
    k_win = jnp.pad(kw_raw, ((0, 0), (WIN, 0), (0, 0), (0, 0)))
    v_win = jnp.pad(vw_raw, ((0, 0), (WIN, 0), (0, 0), (0, 0)))

    def block(qi):
        q0 = qi * Q_BLOCK
        t = q0 + jnp.arange(Q_BLOCK)
        qb = lax.dynamic_slice_in_dim(q, q0, Q_BLOCK, axis=1)
        gb = lax.dynamic_slice_in_dim(gates, q0, Q_BLOCK, axis=1)
        s_c = jnp.einsum("bqgrd,bngd->bqgrn", qb, k_cmp) * scale
        m_c = (cmp_last[None, :] <= t[:, None])[None, :, None, None, :]
        p_c, _, _ = _masked_softmax(s_c, m_c)
        o_c = jnp.einsum("bqgrn,bngd->bqgrd", p_c.astype(qb.dtype), v_cmp)
        imp = jnp.einsum("bqgrn,ns->bqgs", p_c, overlap)
        blk = jnp.arange(n_slc)[None, :]
        cur = (t // SLC_BLOCK)[:, None]
        forced = (blk == 0) | (blk == cur) | (blk == cur - 1)
        causal = blk * SLC_BLOCK <= t[:, None]
        imp = jnp.where(forced[None, :, None, :], FORCE_SCORE, imp)
        imp = jnp.where(causal[None, :, None, :], imp, -1.0)
        _, sel = lax.top_k(imp, top)
        k_sel = k_slc[b_ix, g_ix, sel]
        v_sel = v_slc[b_ix, g_ix, sel].reshape(B, Q_BLOCK, G, top * SLC_BLOCK, E)
        pos = sel[..., None] * SLC_BLOCK + jnp.arange(SLC_BLOCK)
        m_s = (pos <= t[None, :, None, None, None]).reshape(B, Q_BLOCK, G, 1, top * SLC_BLOCK)
        s_s = jnp.einsum("bqgrd,bqgkld->bqgrkl", qb, k_sel).reshape(
            B, Q_BLOCK, G, R, top * SLC_BLOCK) * scale
        p_s, _, _ = _masked_softmax(s_s, m_s)
        o_s = jnp.einsum("bqgrk,bqgkd->bqgrd", p_s.astype(qb.dtype), v_sel)
        kw = lax.dynamic_slice_in_dim(k_win, q0, WIN + Q_BLOCK, axis=1)
        vw = lax.dynamic_slice_in_dim(v_win, q0, WIN + Q_BLOCK, axis=1)
        kpos = q0 - WIN + jnp.arange(WIN + Q_BLOCK)
        dist = t[:, None] - kpos[None, :]
        m_w = ((dist >= 0) & (dist < WIN) & (kpos[None, :] >= 0))[None, :, None, None, :]
        s_w = jnp.einsum("bqgrd,bkgd->bqgrk", qb, kw) * scale
        p_w, _, _ = _masked_softmax(s_w, m_w)
        o_w = jnp.einsum("bqgrk,bkgd->bqgrd", p_w.astype(qb.dtype), vw)
        return gb[..., 0:1] * o_c + gb[..., 1:2] * o_s + gb[..., 2:3] * o_w

    out = lax.map(block, jnp.arange(T // Q_BLOCK))
    return jnp.moveaxis(out, 0, 1).reshape(B, T, G * R * E)


def _dilated_group(q, k, v, window, dil):
    B, T, H, E = q.shape
    L = T // dil
    Lp = -(-L // DIL_BLOCK) * DIL_BLOCK
    nb = Lp // DIL_BLOCK
    steps = window // dil

    def sub(a):
        a = a.reshape(B, L, dil, H, E).transpose(0, 2, 3, 1, 4)
        a = jnp.pad(a, ((0, 0), (0, 0), (0, 0), (0, Lp - L), (0, 0)))
        return a.reshape(B, dil, H, nb, DIL_BLOCK, E)

    def with_prev(a):
        prev = jnp.concatenate([jnp.zeros_like(a[:, :, :, :1]), a[:, :, :, :-1]], axis=3)
        return jnp.concatenate([prev, a], axis=4)

    qs = sub(q)
    kc = with_prev(sub(k))
    vc = with_prev(sub(v))
    s = jnp.einsum("bdhnqe,bdhnke->bdhnqk", qs, kc) * (E ** -0.5)
    a_i = jnp.arange(DIL_BLOCK)
    c_i = jnp.arange(2 * DIL_BLOCK)
    dist = a_i[:, None] + DIL_BLOCK - c_i[None, :]
    j = jnp.arange(nb)[:, None, None] * DIL_BLOCK - DIL_BLOCK + c_i[None, None, :]
    mask = (dist[None] >= 0) & (dist[None] <= steps) & (j >= 0)
    p, m, den = _masked_softmax(s, mask)
    o = jnp.einsum("bdhnqk,bdhnke->bdhnqe", p.astype(v.dtype), vc)
    lse = (m + jnp.log(den))[..., 0]
    o = o.reshape(B, dil, H, Lp, E)[:, :, :, :L].transpose(0, 3, 1, 2, 4).reshape(B, T, H, E)
    lse = lse.reshape(B, dil, H, Lp)[:, :, :, :L].transpose(0, 3, 1, 2).reshape(B, T, H)
    return o, lse


def _dilated_mixer(q, k, v):
    B, T, _, E = q.shape
    outs, lses = [], []
    for gi, (window, dil) in enumerate(DIL_PAIRS):
        sl = slice(gi * DIL_HEADS_PER_PAIR, (gi + 1) * DIL_HEADS_PER_PAIR)
        o, lse = _dilated_group(q[:, :, sl], k[:, :, sl], v[:, :, sl], window, dil)
        outs.append(o)
        lses.append(lse)
    alpha = jax.nn.softmax(jnp.stack(lses, axis=0), axis=0)
    y = jnp.concatenate([alpha[gi][..., None].astype(outs[gi].dtype) * outs[gi]
                         for gi in range(len(DIL_PAIRS))], axis=2)
    return y.reshape(B, T, DIL_W)


def setup_inputs(seed: int = 0) -> dict:
    key = jax.random.key(seed)
    ks = jax.random.split(key, 13)
    f32 = jnp.float32
    x = jax.random.normal(ks[0], (BATCH, SEQ, D_MODEL), f32)
    p = jax.random.normal(ks[1], (DEPTH, BATCH, SEQ, PLE_DIM), f32)
    norm_mix = 1.0 + 0.05 * jax.random.normal(ks[2], (DEPTH, D_MODEL), f32)
    w_in = jax.random.normal(ks[3], (DEPTH, D_MODEL, D_IN), f32) * D_MODEL ** -0.5
    conv_w = jax.random.normal(ks[4], (DEPTH, CONV_K, CONV_W), f32) * CONV_K ** -0.5
    conv_b = 0.02 * jax.random.normal(ks[5], (DEPTH, CONV_W), f32)
    cmp_pe = 0.5 * jax.random.normal(ks[6], (DEPTH, 2, CMP_BLOCK, HEAD_DIM), f32)
    cmp_w1 = jax.random.normal(ks[7], (DEPTH, 2, CMP_BLOCK * HEAD_DIM, HEAD_DIM), f32) * (CMP_BLOCK * HEAD_DIM) ** -0.5
    cmp_w2 = jax.random.normal(ks[8], (DEPTH, 2, HEAD_DIM, HEAD_DIM), f32) * HEAD_DIM ** -0.5
    w_out = jax.random.normal(ks[9], (DEPTH, D_MIX, D_MODEL), f32) * D_MIX ** -0.5
    w_ple_gate = jax.random.normal(ks[10], (DEPTH, D_MODEL, D_MODEL), f32) * D_MODEL ** -0.5
    w_ple_proj = jax.random.normal(ks[11], (DEPTH, PLE_DIM, D_MODEL), f32) * PLE_DIM ** -0.5
    norm_final = 1.0 + 0.05 * jax.random.normal(ks[12], (D_MODEL,), f32)
    return {"x": x, "p": p, "norm_mix": norm_mix, "w_in": w_in, "conv_w": conv_w,
            "conv_b": conv_b, "cmp_pe": cmp_pe, "cmp_w1": cmp_w1, "cmp_w2": cmp_w2,
            "w_out": w_out, "w_ple_gate": w_ple_gate, "w_ple_proj": w_ple_proj,
            "norm_final": norm_final}


def reference(x, p, norm_mix, w_in, conv_w, conv_b, cmp_pe, cmp_w1, cmp_w2,
              w_out, w_ple_gate, w_ple_proj, norm_final):
    B, T, _ = x.shape
    E = HEAD_DIM
    split_points = np.cumsum(IN_SIZES)[:-1].tolist()
    h = x
    for i in range(DEPTH):
        xn = _rmsnorm(h, norm_mix[i])
        u = xn @ w_in[i]
        (a_b, a_c, a_h, a_z,
         n_q, n_kc, n_vc, n_ks, n_vs, n_kw, n_vw, n_g, n_z,
         d_q, d_k, d_v, d_z) = jnp.split(u, split_points, axis=-1)
        y_a = a_b * _short_conv(a_c * a_h, conv_w[i], conv_b[i]) * jax.nn.silu(a_z)
        kv = lambda a: a.reshape(B, T, NSA_KV_HEADS, E)
        gates = jax.nn.sigmoid(n_g.astype(jnp.float32)).astype(u.dtype).reshape(
            B, T, NSA_KV_HEADS, NSA_GROUP, 3)
        y_b = _nsa(n_q.reshape(B, T, NSA_KV_HEADS, NSA_GROUP, E), kv(n_kc), kv(n_vc),
                   kv(n_ks), kv(n_vs), kv(n_kw), kv(n_vw), gates,
                   cmp_pe[i], cmp_w1[i], cmp_w2[i]) * jax.nn.silu(n_z)
        hd = lambda a: a.reshape(B, T, DIL_HEADS, E)
        y_c = _dilated_mixer(hd(d_q), hd(d_k), hd(d_v)) * jax.nn.silu(d_z)
        h = h + jnp.concatenate([y_a, y_b, y_c], axis=-1) @ w_out[i]
        gate = jax.nn.sigmoid((h @ w_ple_gate[i]).astype(jnp.float32)).astype(h.dtype)
        h = h + gate * (p[i] @ w_ple_proj[i])
    return _rmsnorm(h, norm_final)
```

```python
from contextlib import ExitStack
import numpy as np
import concourse.bass as bass
import concourse.mybir as mybir
from concourse.bass_utils import run_bass_kernel_spmd

F32 = mybir.dt.float32
BF16 = mybir.dt.bfloat16
AF = mybir.ActivationFunctionType
ALU = mybir.AluOpType

D_MODEL = 1024
PLE_DIM = 256
EPS = 1e-6
NEG = -30000.0
SCALE = 0.125
NFM = 19
NTM = 330
D_IN = NFM * 128 + NTM
D_INP = (D_IN + 15) // 16 * 16
NYC = 5


class Buf:
    def __init__(self, name):
        self.name = name
        self.w = {}
        self.r = {}
        self.slots = None
        self.dn = 0
        self.dq = None


class Eng:
    def __init__(self, key, eng, sem, selfsync):
        self.key, self.eng, self.sem, self.selfsync = key, eng, sem, selfsync
        self.n = 0
        self.waited = {}


class Ctx:
    def __init__(self, nc, es):
        self.nc = nc
        self.es = es
        self.engs = {}
        for key, eng, ss in (("pe", nc.tensor, False), ("act", nc.scalar, True),
                             ("dve", nc.vector, True), ("pool", nc.gpsimd, True),
                             ("sp", nc.sync, False)):
            sem = es.enter_context(nc.semaphore("s_" + key))
            self.engs[key] = Eng(key, eng, sem, ss)
        self.free_slots = []
        self.extra_toks = {}
        self.n_dsems = 0
        self.live_dma_bufs = []

    def buf(self, name):
        return Buf(name)

    def _new_slot(self):
        self.n_dsems += 1
        key = "d%d" % self.n_dsems
        return [key, self.es.enter_context(self.nc.semaphore(key)), 0]

    def _dsem(self, b, qkey):
        if b.slots is None:
            b.dq = qkey
            if qkey == "sp":
                b.slots = [self.free_slots.pop() if self.free_slots else self._new_slot() for _ in range(2)]
            else:
                b.slots = [self._new_slot() for _ in range(3)]
            self.live_dma_bufs.append(b)

    def _wait(self, e, tok):
        k, sem, v = tok
        if k == e.key and not e.selfsync:
            return
        if e.waited.get(k, 0) >= v:
            return
        e.eng.wait_ge(sem, v)
        e.waited[k] = v

    def _sync(self, e, reads, writes, waw):
        deps = {}

        def add(tok):
            if deps.get(tok[0], (None, None, 0))[2] < tok[2]:
                deps[tok[0]] = tok
        for b in reads:
            for t in b.w.values():
                add(t)
        for b in writes:
            if waw:
                for t in b.w.values():
                    add(t)
            for t in b.r.values():
                add(t)
        for tok in deps.values():
            self._wait(e, tok)

    def op(self, ekey, fn, reads=(), writes=(), waw=True):
        e = self.engs[ekey]
        self._sync(e, reads, writes, waw)
        ins = fn(e.eng)
        e.n += 1
        ins.then_inc(e.sem, 1)
        tok = (e.key, e.sem, e.n)
        for b in reads:
            b.r[e.key] = tok
        for b in writes:
            b.w[e.key] = tok
        return ins

    def dma(self, qkey, out, in_, src, dst, waw=True):
        e = self.engs[qkey]
        self._sync(e, [src] if src is not None else [], [dst], waw)
        self._dsem(dst, qkey)
        assert dst.dq == qkey
        slot = dst.slots[dst.dn % len(dst.slots)]
        dst.dn += 1
        if slot[2] > 0:
            self._wait(e, (slot[0], slot[1], 16 * slot[2]))
        slot[2] += 1
        e.eng.dma_start(out=out, in_=in_).then_inc(slot[1], 16)
        tok = (slot[0], slot[1], 16 * slot[2])
        dst.w[slot[0]] = tok
        if src is not None:
            src.r[slot[0]] = tok

    def barrier(self, release=()):
        toks = [(e.key, e.sem, e.n) for e in self.engs.values() if e.n > 0]
        toks += list(self.extra_toks.values())
        for b in self.live_dma_bufs:
            toks += [(s_[0], s_[1], 16 * s_[2]) for s_ in b.slots if s_[2] > 0]
        for e in self.engs.values():
            for tok in toks:
                if tok[0] == e.key:
                    continue
                self._wait(e, tok)
        for b in release:
            if b.slots is not None and b.dq == "sp":
                self.free_slots += b.slots
                self.live_dma_bufs.remove(b)
                b.slots = None


class TL:
    uid = 0

    def __init__(self, ctx, stack, name, shape, dtype, psum=False):
        TL.uid += 1
        nm = "%s_%d" % (name, TL.uid)
        if psum:
            esz = 4 if dtype == F32 else 2
            n = 1
            for d_ in shape[1:]:
                n *= d_
            assert shape[0] == 128
            if n * esz > 2048:
                assert n * esz == 4096 and len(shape) == 3 and shape[1] == 2
                full = stack.enter_context(ctx.nc.psum_tensor(nm, [128, 4096 // esz], dtype))
            else:
                full = stack.enter_context(ctx.nc.psum_tensor(nm, [128, 2048 // esz], dtype))
            v = full[:, 0:n]
            if len(shape) == 3:
                v = v.rearrange("p (a b) -> p a b", a=shape[1])
            elif len(shape) == 4:
                v = v.rearrange("p (a b c) -> p a b c", a=shape[1], b=shape[2])
            self.t = v
        else:
            self.t = stack.enter_context(ctx.nc.sbuf_tensor(nm, list(shape), dtype))
        self.b = ctx.buf(name)

    def __getitem__(self, idx):
        return self.t[idx]


class DT:
    def __init__(self, ctx, name, shape, dtype, kind=None):
        if kind is None:
            self.t = ctx.nc.dram_tensor(name, list(shape), dtype)
        else:
            self.t = ctx.nc.dram_tensor(name, list(shape), dtype, kind=kind)
        self.b = ctx.buf(name)

    def ap(self):
        return self.t.ap()


def build(T, depth, debug=False, phases="ABCDXE", nlayers=None, n_pairs=4):
    nc = bass.Bass("TRN2", target_bir_lowering=False)
    NT = T // 128
    NTT = T // 512
    NS = T // 64
    NCC = max(1, T // 2048)
    NCMP = T // 16 - 1
    es = ExitStack()
    cx = Ctx(nc, es)
    op, dma = cx.op, cx.dma
    skind = "ExternalOutput" if debug else None

    def ext_in(name, shape):
        return DT(cx, name, shape, F32, kind="ExternalInput")

    xT = ext_in("xT", [D_MODEL, T])
    pT = ext_in("pT", [depth, PLE_DIM, T])
    win = ext_in("win", [depth, 128, 8, D_IN])
    gin = ext_in("gin", [depth, 128, 8])
    convw = ext_in("convw", [depth, 128, 1, 4])
    w1blk = ext_in("w1blk", [depth, 2, 128, 32, 128])
    w2blk = ext_in("w2blk", [depth, 2, 128, 128])
    peT = ext_in("peT", [depth, 2, 128, 32])
    wout = ext_in("wout", [depth, 128, 2 * NYC, D_MODEL])
    wgate = ext_in("wgate", [depth, 128, 8, D_MODEL])
    wproj = ext_in("wproj", [depth, 128, 2, D_MODEL])
    gfin = ext_in("gfin", [128, 8])
    c_ident = ext_in("c_ident", [128, 128])
    c_causal = ext_in("c_causal", [128, 128])
    c_winold = ext_in("c_winold", [128, 128])
    c_dprev = ext_in("c_dprev", [128, 128])
    c_cmpb = ext_in("c_cmpb", [128, 17, 128])
    c_wexp = ext_in("c_wexp", [128, T])
    c_ovl = ext_in("c_ovl", [128, NCC, NS])
    c_keep = ext_in("c_keep", [128, 2 * NS])
    c_add = ext_in("c_add", [128, 2 * NS])
    outT = DT(cx, "outT", [D_MODEL, T], F32, kind="ExternalOutput")

    hT = DT(cx, "hT", [D_MODEL, T], F32, kind=skind)
    yTs = [[DT(cx, "yT%d_%d" % (l, t_), [NYC * 128, 512], BF16) for t_ in range(NTT)] for l in range(depth)]
    yTfs = [[DT(cx, "yTf%d_%d" % (l, t_), [2 * NYC * 128, 512], BF16) for t_ in range(NTT)]
            for l in range(depth)]
    for l in range(depth):
        for t_ in range(1, NTT):
            yTs[l][t_].b = yTs[l][0].b
    QNd = DT(cx, "QNd", [128, 3, T], BF16, kind=skind)
    KSWd = DT(cx, "KSWd", [128, 2, T], BF16, kind=skind)
    SZd = DT(cx, "SZd", [128, 4, T], BF16, kind=skind)
    DQKd = DT(cx, "DQKd", [128, 4, T], BF16, kind=skind)
    VSWd = DT(cx, "VSWd", [T, 2, 66], BF16, kind=skind)
    DVd = DT(cx, "DVd", [T, 3, 66], BF16, kind=skind)
    DOd = DT(cx, "DOd", [T, 3, 65], F32, kind=skind)
    ccsem = es.enter_context(nc.semaphore("ccsem"))
    ccn = [0]

    def fm(d):
        return d.ap().rearrange("(c p) t -> p c t", p=128)

    gs = es
    ident_b = TL(cx, gs, "ident_b", [128, 128], BF16)
    causal3 = TL(cx, gs, "causal3", [128, 384], BF16)
    winold3 = TL(cx, gs, "winold3", [128, 384], BF16)
    dilb = TL(cx, gs, "dilb", [128, 512], BF16)
    ones_f = TL(cx, gs, "ones_f", [128, 128], F32)
    keepW = TL(cx, gs, "keepW", [128, 2 * NS], F32)
    addW = TL(cx, gs, "addW", [128, 2 * NS], F32)
    gfin_s = TL(cx, gs, "gfin_s", [128, 8], F32)
    eps_t = TL(cx, gs, "eps_t", [128, 1], F32)
    e0f = TL(cx, gs, "e0f", [128, 1], F32)

    with ExitStack() as ss:
        st = [TL(cx, ss, "cst%d" % i, [128, 128], F32) for i in range(4)]
        for i, src in enumerate((c_ident, c_causal, c_winold, c_dprev)):
            dma("sp", st[i][:, :], src.ap()[:, :], src.b, st[i].b)
        dma("sp", keepW[:, :], c_keep.ap()[:, :], c_keep.b, keepW.b)
        dma("sp", addW[:, :], c_add.ap()[:, :], c_add.b, addW.b)
        dma("sp", gfin_s[:, :], gfin.ap()[:, :], gfin.b, gfin_s.b)
        op("dve", lambda e: e.tensor_copy(ident_b[:, :], st[0][:, :]), [st[0].b], [ident_b.b])
        op("dve", lambda e: e.tensor_copy(e0f[:, :], st[0][:, 0:1]), [st[0].b], [e0f.b])
        for r in range(3):
            op("dve", lambda e, r=r: e.tensor_copy(causal3[:, r * 128:(r + 1) * 128], st[1][:, :]),
               [st[1].b], [causal3.b])
            op("dve", lambda e, r=r: e.tensor_copy(winold3[:, r * 128:(r + 1) * 128], st[2][:, :]),
               [st[2].b], [winold3.b])
        for k in range(4):
            s = st[3] if k % 2 == 0 else st[1]
            op("dve", lambda e, k=k, s=s: e.tensor_copy(dilb[:, k * 128:(k + 1) * 128], s[:, :]),
               [s.b], [dilb.b])
        op("pool", lambda e: e.memset(ones_f[:, :], 1.0), [], [ones_f.b])
        op("pool", lambda e: e.memset(eps_t[:, :], EPS), [], [eps_t.b])
        cx.barrier(release=[s.b for s in st])

    MUL, ADD = ALU.mult, ALU.add

    def mm(out, lhsT, rhs, start, stop, reads, writes, skip=False):
        op("pe", lambda e: e.matmul(out, lhsT, rhs, start=start, stop=stop, skip_group_check=skip), reads, writes)

    def phaseA(L, G_res, KVC):
        src_h = xT if L == 0 else hT
        with ExitStack() as ps:
            wb = TL(cx, ps, "wb", [128, 8, D_INP], BF16)
            gsb = TL(cx, ps, "gsb", [128, 8], F32)
            cw = TL(cx, ps, "cw", [128, 1, 4], F32)
            dma("sp", gsb[:, :], gin.ap()[L, :, :], gin.b, gsb.b)
            dma("sp", cw[:, :, :], convw.ap()[L, :, :, :], convw.b, cw.b)
            with ExitStack() as ws:
                wst = [TL(cx, ws, "wst%d" % i, [128, D_IN], F32) for i in range(2)]
                for kc in range(8):
                    s = wst[kc % 2]
                    dma("sp", s[:, :], win.ap()[L, :, kc, :], win.b, s.b)
                    op("dve", lambda e, kc=kc, s=s: e.tensor_scalar(
                        wb[:, kc, 0:1856], s[:, 0:1856], gsb[:, kc:kc + 1], None, op0=MUL),
                        [s.b, gsb.b], [wb.b], waw=False)
                    op("act", lambda e, kc=kc, s=s: e.activation(
                        wb[:, kc, 1856:D_IN], s[:, 1856:D_IN], AF.Identity, scale=gsb[:, kc:kc + 1]),
                        [s.b, gsb.b], [wb.b], waw=False)
                cx.barrier(release=[s.b for s in wst])

            ht = TL(cx, ps, "ht", [128, 8, 512], F32)
            sqc = [TL(cx, ps, "sqc%d" % i, [128, 512], F32) for i in range(2)]
            xb = [TL(cx, ps, "xb%d" % i, [128, 8, 512], BF16) for i in range(2)]
            rbc = TL(cx, ps, "rbc", [128, 512], F32)
            rtm = TL(cx, ps, "rtm", [128, 4], F32)
            tb = TL(cx, ps, "tb", [128, 512], F32)
            t1 = TL(cx, ps, "t1", [128, 512], F32)
            th = TL(cx, ps, "th", [128, 512], F32)
            tz = TL(cx, ps, "tz", [128, 512], F32)
            tzz = TL(cx, ps, "tzz", [128, 512], F32)
            ch = [TL(cx, ps, "ch%d" % i, [128, 514], F32) for i in range(1)]
            sQ = TL(cx, ps, "sQ", [128, 3, 512], BF16)
            sK = TL(cx, ps, "sK", [128, 2, 512], BF16)
            sZ = TL(cx, ps, "sZ", [128, 4, 512], BF16)
            sD = TL(cx, ps, "sD", [128, 4, 512], BF16)
            sY = TL(cx, ps, "sY", [128, 1, 512], BF16)
            gtmp = TL(cx, ps, "gtmp", [128, 4, 10], F32)
            sV = TL(cx, ps, "sV", [128, 4, 2, 66], BF16)
            sDV = TL(cx, ps, "sDV", [128, 4, 3, 66], BF16)
            pa = [TL(cx, ps, "pa%d" % i, [128, 512], F32, psum=True) for i in range(3)]
            ptm0 = TL(cx, ps, "ptm0", [128, 512], F32, psum=True)
            ptm1 = TL(cx, ps, "ptm1", [128, 512], F32, psum=True)
            pst = TL(cx, ps, "pst", [128, 512], F32, psum=True)
            pst2 = TL(cx, ps, "pst2", [128, 4], F32, psum=True)

            for c in ch:
                op("pool", lambda e, c=c: e.memset(c[:, :], 0.0), [], [c.b])
            op("pool", lambda e: e.memset(sV[:, :, :, :], 1.0), [], [sV.b])
            op("pool", lambda e: e.memset(sDV[:, :, :, :], 1.0), [], [sDV.b])

            pai = [0]

            def proj_chunk(j, xbt):
                p = pa[pai[0] % 3]
                pai[0] += 1
                for kc in range(8):
                    mm(p[:, :], wb[:, kc, j * 128:(j + 1) * 128], xbt[:, kc, :], kc == 0, kc == 7,
                       [wb.b, xbt.b], [p.b])
                return p

            def evac_mul(p, dst_ap, dst_b):
                op("dve", lambda e: e.tensor_tensor(dst_ap, p[:, :], rbc[:, :], MUL),
                   [p.b, rbc.b], [dst_b], waw=False)

            def evac_silu(p, dst_ap, dst_b):
                op("dve", lambda e: e.tensor_tensor(tzz[:, :], p[:, :], rbc[:, :], MUL),
                   [p.b, rbc.b], [tzz.b])
                op("act", lambda e: e.activation(dst_ap, tzz[:, :], AF.Silu), [tzz.b], [dst_b], waw=False)

            import os as _osA
            _astep = int(_osA.environ.get("KDBG_A", "9"))
            for tt in range(NTT if _astep >= 1 else 0):
                t0 = tt * 512
                xbt = xb[tt % 2]
                dma("sp", ht[:, :, :], fm(src_h)[:, :, t0:t0 + 512], src_h.b, ht.b)
                for kc in range(8):
                    sq = sqc[kc % 2]
                    op("act", lambda e, kc=kc, sq=sq: e.activation(sq[:, :], ht[:, kc, :], AF.Square),
                       [ht.b], [sq.b])
                    mm(pst[:, :], ones_f[:, :], sq[:, :], kc == 0, kc == 7, [ones_f.b, sq.b], [pst.b])
                op("act", lambda e, xbt=xbt: e.copy(xbt[:, :, :], ht[:, :, :]), [ht.b], [xbt.b])
                op("dve", lambda e: e.tensor_scalar(
                    rbc[:, :], pst[:, :], 1.0 / D_MODEL, EPS, op0=MUL, op1=ADD), [pst.b], [rbc.b])
                op("act", lambda e: e.activation(rbc[:, :], rbc[:, :], AF.Sqrt), [rbc.b], [rbc.b])
                op("dve", lambda e: e.reciprocal(rbc[:, :], rbc[:, :]), [rbc.b], [rbc.b])
                for sub in range(4):
                    mm(pst2[:, sub:sub + 1], rbc[:, sub * 128:(sub + 1) * 128], e0f[:, 0:1],
                       True, True, [rbc.b, e0f.b], [pst2.b])
                op("dve", lambda e: e.tensor_copy(rtm[:, :], pst2[:, :]), [pst2.b], [rtm.b])

                if _astep < 2:
                    continue
                for cc in range(1):
                    p = proj_chunk(0, xbt)
                    evac_mul(p, tb[:, :], tb.b)
                    p = proj_chunk(1, xbt)
                    evac_mul(p, t1[:, :], t1.b)
                    p = proj_chunk(2, xbt)
                    evac_mul(p, th[:, :], th.b)
                    p = proj_chunk(3, xbt)
                    evac_mul(p, tz[:, :], tz.b)
                    op("act", lambda e: e.activation(tz[:, :], tz[:, :], AF.Silu), [tz.b], [tz.b])
                    c = ch[cc]
                    op("pool", lambda e, c=c: e.tensor_tensor(c[:, 2:514], t1[:, :], th[:, :], MUL),
                       [t1.b, th.b], [c.b])
                    op("dve", lambda e, c=c, cc=cc: e.tensor_scalar(
                        t1[:, :], c[:, 2:514], cw[:, cc, 2:3], cw[:, cc, 3:4], op0=MUL, op1=ADD),
                        [c.b, cw.b], [t1.b])
                    op("dve", lambda e, c=c, cc=cc: e.scalar_tensor_tensor(
                        t1[:, :], c[:, 1:513], cw[:, cc, 1:2], t1[:, :], op0=MUL, op1=ADD),
                        [c.b, cw.b, t1.b], [t1.b])
                    op("dve", lambda e, c=c, cc=cc: e.scalar_tensor_tensor(
                        t1[:, :], c[:, 0:512], cw[:, cc, 0:1], t1[:, :], op0=MUL, op1=ADD),
                        [c.b, cw.b, t1.b], [t1.b])
                    op("pool", lambda e: e.tensor_tensor(t1[:, :], t1[:, :], tb[:, :], MUL),
                       [t1.b, tb.b], [t1.b])
                    op("pool", lambda e, cc=cc: e.tensor_tensor(sY[:, cc, :], t1[:, :], tz[:, :], MUL),
                       [t1.b, tz.b], [sY.b])
                    op("pool", lambda e, c=c: e.tensor_copy(c[:, 0:2], c[:, 512:514]), [c.b], [c.b])
                dma("pool", fm(yTs[L][tt])[:, 0:1, :], sY[:, :, :], sY.b, yTs[L][tt].b, waw=False)

                if _astep < 3:
                    continue
                for j in (4, 5, 6):
                    evac_mul(proj_chunk(j, xbt), sQ[:, j - 4, :], sQ.b)
                dma("pool", QNd.ap()[:, :, t0:t0 + 512], sQ[:, :, :], sQ.b, QNd.b, waw=False)
                for j in (7, 8):
                    evac_mul(proj_chunk(j, xbt), KVC[:, j - 7, t0:t0 + 512], KVC.b)
                for j in (9, 10):
                    evac_mul(proj_chunk(j, xbt), sK[:, j - 9, :], sK.b)
                dma("pool", KSWd.ap()[:, :, t0:t0 + 512], sK[:, :, :], sK.b, KSWd.b, waw=False)
                for j in (11, 12, 17, 18):
                    zi = j - 11 if j < 13 else 2 + j - 17
                    evac_silu(proj_chunk(j, xbt), sZ[:, zi, :], sZ.b)
                dma("pool", SZd.ap()[:, :, t0:t0 + 512], sZ[:, :, :], sZ.b, SZd.b, waw=False)
                for j in range(13, 17):
                    evac_mul(proj_chunk(j, xbt), sD[:, j - 13, :], sD.b)
                dma("pool", DQKd.ap()[:, :, t0:t0 + 512], sD[:, :, :], sD.b, DQKd.b, waw=False)

                if _astep < 4:
                    continue
                for sub in range(4):
                    for (pt, c0, c1) in ((ptm0, 0, NTM),):
                        for kc in range(8):
                            mm(pt[:, 0:c1 - c0], xbt[:, kc, sub * 128:(sub + 1) * 128],
                               wb[:, kc, NFM * 128 + c0:NFM * 128 + c1], kc == 0, kc == 7,
                               [wb.b, xbt.b], [pt.b])
                    rs = rtm[:, sub:sub + 1]
                    if _astep < 5:
                        continue
                    op("dve", lambda e, sub=sub, rs=rs: e.tensor_scalar(
                        sV[:, sub, :, 0:64], ptm0[:, 0:128].rearrange("p (a e) -> p a e", e=64),
                        rs, None, op0=MUL), [ptm0.b, rtm.b], [sV.b], waw=False)
                    op("dve", lambda e, sub=sub, rs=rs: e.tensor_scalar(
                        sDV[:, sub, :, 0:64], ptm0[:, 128:320].rearrange("p (a e) -> p a e", e=64),
                        rs, None, op0=MUL), [ptm0.b, rtm.b], [sDV.b], waw=False)
                    if _astep < 6:
                        continue
                    op("dve", lambda e, sub=sub, rs=rs: e.tensor_scalar(
                        gtmp[:, sub, :], ptm0[:, 320:330], rs, None, op0=MUL), [ptm0.b, rtm.b], [gtmp.b],
                        waw=False)
                    op("act", lambda e, sub=sub, tt=tt: e.activation(
                        G_res[:, tt * 4 + sub, :], gtmp[:, sub, :], AF.Sigmoid),
                        [gtmp.b], [G_res.b], waw=False)
                if _astep < 7:
                    continue
                dma("pool", VSWd.ap()[t0:t0 + 512, :, :].rearrange("(s p) b e -> p s b e", p=128),
                    sV[:, :, :, :], sV.b, VSWd.b, waw=False)
                dma("pool", DVd.ap()[t0:t0 + 512, :, :].rearrange("(s p) b e -> p s b e", p=128),
                    sDV[:, :, :, :], sDV.b, DVd.b, waw=False)
            cx.barrier(release=[ht.b, sQ.b, sK.b, sZ.b, sD.b, sY.b, sV.b, sDV.b, wb.b, gsb.b, cw.b])

    def phaseB(L, KVC, KCMPT, VCX):
        with ExitStack() as ps:
            w1s = TL(cx, ps, "w1s", [128, 32, 128], F32)
            w1b = [TL(cx, ps, "w1b%d" % k, [128, 32, 128], BF16) for k in range(2)]
            w2s = TL(cx, ps, "w2s", [128, 2, 128], F32)
            w2b = TL(cx, ps, "w2b", [128, 2, 128], BF16)
            pes = TL(cx, ps, "pes", [128, 2, 32], F32)
            peb = TL(cx, ps, "peb", [128, 2, 32], BF16)
            cb = TL(cx, ps, "cb", [128, 2], F32)
            hid = TL(cx, ps, "hid", [128, NCC * 128], BF16)
            ovs = TL(cx, ps, "ovs", [128, NCC, NS], F32)
            pcm = TL(cx, ps, "pcm", [128, 512], F32, psum=True)
            pcb = TL(cx, ps, "pcb", [128, 2], F32, psum=True)
            pk = TL(cx, ps, "pk", [128, 512], F32, psum=True)
            pv = TL(cx, ps, "pv", [128, 128], F32, psum=True)
            for kv in range(2):
                dma("sp", w1s[:, :, :], w1blk.ap()[L, kv, :, :, :], w1blk.b, w1s.b)
                op("dve", lambda e, kv=kv: e.tensor_copy(w1b[kv][:, :, :], w1s[:, :, :]), [w1s.b], [w1b[kv].b])
            dma("sp", w2s[:, :, :], w2blk.ap()[L, :, :, :].rearrange("k p f -> p k f"), w2blk.b, w2s.b)
            op("dve", lambda e: e.tensor_copy(w2b[:, :, :], w2s[:, :, :]), [w2s.b], [w2b.b])
            dma("sp", pes[:, :, :], peT.ap()[L, :, :, :].rearrange("k p l -> p k l"), peT.b, pes.b)
            op("dve", lambda e: e.tensor_copy(peb[:, :, :], pes[:, :, :]), [pes.b], [peb.b])
            op("pool", lambda e: e.memset(hid[:, :], 0.0), [], [hid.b])
            dma("sp", ovs[:, :, :], c_ovl.ap()[:, :, :], c_ovl.b, ovs.b)
            op("pool", lambda e: e.memset(VCX[:, :, :, :], 1.0), [], [VCX.b])
            for g in range(2):
                op("dve", lambda e, g=g: e.tensor_copy(VCX[:, :, g, 65:65 + NS], ovs[:, :, :]),
                   [ovs.b], [VCX.b])
            for kv in range(2):
                for l in range(32):
                    mm(pcb[:, kv:kv + 1], w1b[kv][:, l, :], peb[:, kv, l:l + 1], l == 0, l == 31,
                       [w1b[kv].b, peb.b], [pcb.b])
            op("dve", lambda e: e.tensor_copy(cb[:, :], pcb[:, :]), [pcb.b], [cb.b])
            for kv in range(2):
                for l in range(32):
                    mm(pcm[:, 0:NCMP], w1b[kv][:, l, :], KVC[:, kv, l:l + 16 * (NCMP - 1) + 1:16],
                       l == 0, l == 31, [w1b[kv].b, KVC.b], [pcm.b])
                op("act", lambda e, kv=kv: e.activation(hid[:, 0:NCMP], pcm[:, 0:NCMP], AF.Silu,
                                                        bias=cb[:, kv:kv + 1]),
                   [pcm.b, cb.b], [hid.b])
                if kv == 0:
                    mm(pk[:, 0:NCC * 128], w2b[:, 0, :], hid[:, :], True, True, [w2b.b, hid.b], [pk.b])
                    op("dve", lambda e: e.tensor_copy(KCMPT[:, :], pk[:, 0:NCC * 128]), [pk.b], [KCMPT.b])
                else:
                    for c in range(NCC):
                        mm(pv[:, :], hid[:, c * 128:(c + 1) * 128], w2b[:, 1, :], True, True,
                           [w2b.b, hid.b], [pv.b])
                        op("dve", lambda e, c=c: e.tensor_copy(
                            VCX[:, c, :, 0:64], pv[:, :].rearrange("p (g e) -> p g e", e=64)),
                            [pv.b], [VCX.b])
            cx.barrier(release=[w1s.b, w2s.b, pes.b, ovs.b])

    def run_jobs(jobs, S, PT, depth_=2):
        n = len(jobs)
        nS, nP = len(S), len(PT)
        assert nS > depth_ and nP > depth_
        for i in range(n + depth_):
            if i < n:
                j = jobs[i]
                s = S[i % nS]
                bias = j.get("bias") or []
                for (c0, c1, kT, q, rd) in j["qk"]:
                    mm(s[:, c0:c1], kT, q, True, not bias, rd, [s.b])
                for bi, (c0, c1, bl, br, brd) in enumerate(bias):
                    mm(s[:, c0:c1], bl, br, False, bi == len(bias) - 1, brd, [s.b])
            k = i - depth_
            if k >= 0:
                j = jobs[k]
                s = S[k % nS]
                pt = PT[k % nP]
                ncol = j["ncol"]
                op("act", lambda e, s=s, pt=pt, ncol=ncol: e.activation(
                    pt[:, 0:ncol], s[:, 0:ncol], AF.Exp, scale=SCALE), [s.b], [pt.b])
                for (out_ap, a, b, rhs_ap, st_, sp_, obuf, rbufs) in j["pv"]:
                    mm(out_ap, pt[:, a:b], rhs_ap, False, sp_, [pt.b] + rbufs, [obuf], skip=True)
                if j.get("post"):
                    j["post"]()
            k = i - depth_ + 1
            if 0 <= k < n and jobs[k].get("pre"):
                jobs[k]["pre"]()

    def run_jobs_d(jobs, S2, PT, depth_=2):
        n = len(jobs)
        nS, nP = len(S2), len(PT)
        assert nS > depth_ and nP > depth_
        for i in range(n + depth_):
            if i < n:
                j = jobs[i]
                s = S2[i % nS]
                hf = j["hf"]
                for qi, (h, c0, c1, kT, q, rd) in enumerate(j["qk"]):
                    mm(s[:, h, c0:c1], kT, q, qi == 0, False, rd, [s.b], skip=True)
                mm(s[:, hf, 0:256], ident_b[:, :], dilb[:, 0:256], False, True, [ident_b.b, dilb.b], [s.b],
                   skip=True)
            k = i - depth_
            if k >= 0:
                j = jobs[k]
                s = S2[k % nS]
                pt = PT[k % nP]
                hf = j["hf"]
                op("act", lambda e, s=s, pt=pt, hf=hf: e.activation(
                    pt[:, 0:256], s[:, hf, 0:256], AF.Exp, scale=SCALE), [s.b], [pt.b])
                for (out_ap, a, b, rhs_ap, st_, sp_, obuf, rbufs) in j["pv"]:
                    mm(out_ap, pt[:, a:b], rhs_ap, False, sp_, [pt.b] + rbufs, [obuf], skip=True)
                if j.get("post"):
                    j["post"]()
            k = i - depth_ + 1
            if 0 <= k < n and jobs[k].get("pre"):
                jobs[k]["pre"]()

    def phaseC(L, G_res, KCMPT, VCX):
        with ExitStack() as ps:
            QN = TL(cx, ps, "QN", [128, 3, T], BF16)
            KSW = TL(cx, ps, "KSW", [128, 2, T], BF16)
            VSW = TL(cx, ps, "VSW", [128, NT, 2, 66], BF16)
            cmpb = TL(cx, ps, "cmpb", [128, 17, 128], BF16)
            with ExitStack() as ss:
                wes = TL(cx, ss, "wes", [128, 2048], F32)
                cms = TL(cx, ss, "cms", [128, 17, 128], F32)
                dma("sp", cms[:, :, :], c_cmpb.ap()[:, :, :], c_cmpb.b, cms.b)
                op("dve", lambda e: e.tensor_copy(cmpb[:, :, :], cms[:, :, :]), [cms.b], [cmpb.b])
                for r in range(3):
                    dma("sp", QN[:, r, :], QNd.ap()[:, r, :], QNd.b, QN.b, waw=False)
                for r in range(2):
                    dma("sp", KSW[:, r, :], KSWd.ap()[:, r, :], KSWd.b, KSW.b, waw=False)
                for k in range(T // 2048):
                    dma("sp", wes[:, :], c_wexp.ap()[:, k * 2048:(k + 1) * 2048], c_wexp.b, wes.b)
                    op("dve", lambda e, k=k: e.tensor_copy(KSW[64:128, 0, k * 2048:(k + 1) * 2048], wes[64:128, :]),
                       [wes.b], [KSW.b])
                vsrc = VSWd.ap().rearrange("(c p) b e -> p c b e", p=128)
                nsp = max(1, NT // 16)
                for k in range(nsp):
                    c0, c1 = k * NT // nsp, (k + 1) * NT // nsp
                    dma("sp", VSW[:, c0:c1, :, :], vsrc[:, c0:c1, :, :], VSWd.b, VSW.b, waw=False)
                cx.barrier(release=[wes.b, cms.b])

            PT = [TL(cx, ps, "PT%d" % i, [128, 512], BF16) for i in range(4)]
            NWIN = max(1, NS // 64)
            MQ = [[TL(cx, ps, "MQ%d_%d" % (w, i), [128, 384], BF16) for i in range(2)] for w in range(NWIN)]
            negq2 = TL(cx, ps, "negq2", [128, 128], BF16)
            op("pool", lambda e: e.memset(negq2[:, :], 0.0), [], [negq2.b])
            rd = [TL(cx, ps, "rd%d" % i, [128, 3, 3], F32) for i in range(2)]
            coef = TL(cx, ps, "coef", [128, 3, 3], F32)
            impf = TL(cx, ps, "impf", [128, NS], F32)
            imp2 = TL(cx, ps, "imp2", [128, NS], F32)
            m8 = TL(cx, ps, "m8", [128, 16], F32)
            negq = TL(cx, ps, "negq", [128, 128], BF16)
            accf = TL(cx, ps, "accf", [128, 64], F32)
            NSAO = [TL(cx, ps, "NSAO%d" % i, [128, 1, 4, 64], BF16) for i in range(2)]
            szt = [TL(cx, ps, "szt%d" % i, [128, 2, 512], BF16) for i in range(2)]
            yst = [TL(cx, ps, "yst%d" % i, [128, 2, 512], BF16) for i in range(2)]
            for t_ in NSAO:
                op("pool", lambda e, t_=t_: e.memset(t_[:, :, :, :], 0.0), [], [t_.b])
            S = [TL(cx, ps, "S%d" % i, [128, 512], F32, psum=True) for i in range(3)]
            IMP = TL(cx, ps, "IMP", [128, 3, 128], F32, psum=True)
            OCW = [TL(cx, ps, "OCW%d" % i, [128, 2, 3, 65], F32, psum=True) for i in range(2)]
            OS = [TL(cx, ps, "OS%d" % i, [128, 3, 65], F32, psum=True) for i in range(1)] * 2
            TP = TL(cx, ps, "TP", [128, 4, 128], BF16, psum=True)
            tpb = [TP.b] * 4
            if NS < 128:
                op("pool", lambda e: e.memset(negq[:, :], 0.0), [], [negq.b])

            def q_ap(g, q0):
                return QN[g * 64:(g + 1) * 64, :, q0:q0 + 128]

            def pv_list(out_tl, out_fn, vfn, c, first, last):
                return [(out_fn(r), r * 128, (r + 1) * 128, vfn(c), first, last, out_tl.b, [VSW.b, VCX.b])
                        for r in range(3)]

            def post_cmp(n, qt, g):
                def f():
                    o = OCW[n % 2]
                    r_ = rd[n % 2]
                    op("dve", lambda e: e.tensor_scalar(r_[:, 0, :], o[:, 0, :, 64], 1e-30, None, op0=ALU.max),
                       [o.b], [r_.b], waw=False)
                    op("dve", lambda e: e.reciprocal(r_[:, 0, :], r_[:, 0, :]), [r_.b], [r_.b])
                    op("dve", lambda e: e.tensor_scalar(impf[:, :], IMP[:, 0, 0:NS], r_[:, 0, 0:1], None, op0=MUL),
                       [IMP.b, r_.b], [impf.b])
                    for r in (1, 2):
                        op("dve", lambda e, r=r: e.scalar_tensor_tensor(
                            impf[:, :], IMP[:, r, 0:NS], r_[:, 0, r:r + 1], impf[:, :], op0=MUL, op1=ADD),
                            [IMP.b, r_.b, impf.b], [impf.b])
                    lo = NS - 2 * qt
                    op("dve", lambda e: e.tensor_tensor(impf[:, :], impf[:, :], keepW[:, lo:lo + NS], MUL),
                       [impf.b, keepW.b], [impf.b])
                    op("dve", lambda e: e.tensor_tensor(impf[:, :], impf[:, :], addW[:, lo:lo + NS], ADD),
                       [impf.b, addW.b], [impf.b])
                    op("dve", lambda e: e.memset(impf[:, 0:1], 1e4), [], [impf.b])
                    op("dve", lambda e: e.max(m8[:, 0:8], impf[:, :]), [impf.b], [m8.b])
                    op("dve", lambda e: e.match_replace(imp2[:, :], m8[:, 0:8], impf[:, :], -1e9),
                       [m8.b, impf.b], [imp2.b])
                    op("dve", lambda e: e.max(m8[:, 8:16], imp2[:, :]), [imp2.b], [m8.b])
                    op("dve", lambda e: e.tensor_scalar(negq[:, 0:NS], impf[:, :], m8[:, 15:16], NEG,
                                                        op0=ALU.is_lt, op1=MUL),
                       [impf.b, m8.b], [negq.b])
                    h0 = min(NS, 64)
                    op("dve", lambda e: e.tensor_scalar(negq2[:, 64:64 + h0], impf[:, 0:h0], m8[:, 15:16], NEG,
                                                        op0=ALU.is_lt, op1=MUL),
                       [impf.b, m8.b], [negq2.b])
                    if NS > 64:
                        op("dve", lambda e: e.tensor_scalar(negq2[:, 0:64], impf[:, 64:128], m8[:, 15:16], NEG,
                                                            op0=ALU.is_lt, op1=MUL),
                           [impf.b, m8.b], [negq2.b], waw=False)
                    op("pe", lambda e: e.transpose(TP[:, 3, :], negq2[:, :], ident_b[:, :]),
                       [negq2.b, ident_b.b], [TP.b])
                    if NWIN > 1:
                        op("pe", lambda e: e.transpose(TP[:, 0, :], negq[:, :], ident_b[:, :]),
                           [negq.b, ident_b.b], [TP.b])
                    q0 = qt * 128
                    for w in range(NWIN):
                        if w * 32 >= qt:
                            continue
                        mq = MQ[w][n % 2]
                        op("pool", lambda e, mq=mq: e.tensor_copy(
                            mq[0:64, :].rearrange("p (r q) -> p r q", r=3), QN[0:64, :, q0:q0 + 128]),
                            [QN.b], [mq.b])
                        slot = 3 if w == 0 else 0
                        for r in range(3):
                            op("dve", lambda e, r=r, mq=mq, slot=slot: e.tensor_copy(
                                mq[64:128, r * 128:(r + 1) * 128], TP[64:128, slot, :]),
                                [TP.b], [mq.b], waw=False)
                return f

            def post_sel(n, qt, g):
                def f():
                    o = OCW[n % 2]
                    os_ = OS[n % 2]
                    r_ = rd[n % 2]
                    op("dve", lambda e: e.tensor_scalar(r_[:, 1, :], os_[:, :, 64], 1e-30, None, op0=ALU.max),
                       [os_.b], [r_.b], waw=False)
                    op("dve", lambda e: e.tensor_scalar(r_[:, 2, :], o[:, 1, :, 64], 1e-30, None, op0=ALU.max),
                       [o.b], [r_.b], waw=False)
                    op("dve", lambda e: e.reciprocal(r_[:, 1:3, :], r_[:, 1:3, :]), [r_.b], [r_.b])
                    gv = G_res[:, qt, g * 9:(g + 1) * 9].rearrange("p (r b) -> p b r", b=3)
                    op("dve", lambda e: e.tensor_tensor(coef[:, :, :], r_[:, :, :], gv, MUL),
                       [r_.b, G_res.b], [coef.b])
                    no = NSAO[qt % 2]
                    for r in range(3):
                        op("dve", lambda e, r=r: e.tensor_scalar(accf[:, :], o[:, 0, r, 0:64], coef[:, 0, r:r + 1],
                                                                  None, op0=MUL), [o.b, coef.b], [accf.b])
                        op("dve", lambda e, r=r: e.scalar_tensor_tensor(
                            accf[:, :], os_[:, r, 0:64], coef[:, 1, r:r + 1], accf[:, :], op0=MUL, op1=ADD),
                            [os_.b, coef.b, accf.b], [accf.b])
                        op("dve", lambda e, r=r: e.scalar_tensor_tensor(
                            no[:, g, r, :], o[:, 1, r, 0:64], coef[:, 2, r:r + 1], accf[:, :], op0=MUL, op1=ADD),
                            [o.b, coef.b, accf.b], [no.b], waw=(r == 0))
                    if True:
                        grp = qt // 4
                        sz = szt[grp % 2]
                        ys = yst[grp % 2]
                        if qt % 4 == 0:
                            t0 = grp * 512
                            dma("sp", sz[:, :, :], SZd.ap()[:, 0:2, t0:t0 + 512], SZd.b, sz.b)
                        nof = no[:, :, :, :].rearrange("p g r e -> p (g r e)")
                        for k in range(2):
                            op("pe", lambda e, k=k: e.transpose(TP[:, 1 + k, :], nof[:, k * 128:(k + 1) * 128],
                                                                 ident_b[:, :]),
                               [no.b, ident_b.b], [tpb[1 + k]])
                        for k in range(2):
                            qo = (qt % 4) * 128
                            op("dve", lambda e, k=k, qo=qo: e.tensor_tensor(
                                ys[:, k, qo:qo + 128], TP[:, 1 + k, :], sz[:, k, qo:qo + 128], MUL),
                                [tpb[1 + k], sz.b], [ys.b], waw=False)
                        if qt % 4 == 3 or qt == NT - 1:
                            t0 = grp * 512
                            dma("pool", fm(yTs[L][grp])[:, 1:3, :], ys[:, :, :], ys.b, yTs[L][grp].b, waw=False)
                return f

            def cmp_jobs(n, qt, g):
                q0 = qt * 128
                jobs = []
                cs = [c for c in range(NCC) if q0 - 2048 * c >= 0]
                for c in cs:
                    d = q0 - 2048 * c
                    bias = []
                    if d < 2176:
                        bias = [(r * 128, (r + 1) * 128, ident_b[:, :], cmpb[:, d // 128, :], [ident_b.b, cmpb.b])
                                for r in range(3)]
                    o = OCW[n % 2]
                    pv = []
                    for r in range(3):
                        pv.append((o[:, 0, r, :], r * 128, (r + 1) * 128, VCX[:, c, g, 0:65],
                                   c == cs[0], c == cs[-1], o.b, [VCX.b]))
                        pv.append((IMP[:, r, 0:NS], r * 128, (r + 1) * 128, VCX[:, c, g, 65:65 + NS],
                                   c == cs[0], c == cs[-1], IMP.b, [VCX.b]))
                    jobs.append(dict(ncol=384, bias=bias, pv=pv,
                                     qk=[(0, 384, KCMPT[g * 64:(g + 1) * 64, c * 128:(c + 1) * 128],
                                          q_ap(g, q0), [KCMPT.b, QN.b])]))
                jobs[-1]["post"] = post_cmp(n, qt, g)

                def pre(o=o):
                    op("dve", lambda e: e.memset(o[:, 0, :, :], 0.0), [], [o.b])
                    op("dve", lambda e: e.memset(IMP[:, :, :], 0.0), [], [IMP.b])
                jobs[0]["pre"] = pre
                return jobs

            def win_jobs(n, qt, g):
                q0 = qt * 128
                jobs = []
                cs = list(range(max(0, qt - 4), qt + 1))
                o = OCW[n % 2]
                for c in cs:
                    bias = []
                    if c == qt:
                        bias = [(0, 384, ident_b[:, :], causal3[:, :], [ident_b.b, causal3.b])]
                    elif c == qt - 4:
                        bias = [(0, 384, ident_b[:, :], winold3[:, :], [ident_b.b, winold3.b])]
                    pv = [(o[:, 1, r, :], r * 128, (r + 1) * 128, VSW[:, c, 1, 0:65], c == cs[0], c == cs[-1],
                           o.b, [VSW.b]) for r in range(3)]
                    jobs.append(dict(ncol=384, bias=bias, pv=pv,
                                     qk=[(0, 384, KSW[g * 64:(g + 1) * 64, 1, c * 128:(c + 1) * 128],
                                          q_ap(g, q0), [KSW.b, QN.b])]))

                def pre(o=o):
                    op("dve", lambda e: e.memset(o[:, 1, :, :], 0.0), [], [o.b])
                jobs[0]["pre"] = pre
                return jobs

            def sel_jobs(n, qt, g):
                q0 = qt * 128
                jobs = []
                o = OS[n % 2]
                for c in range(qt + 1):
                    pv = [(o[:, r, :], r * 128, (r + 1) * 128, VSW[:, c, g, 0:65], c == 0, c == qt,
                           o.b, [VSW.b]) for r in range(3)]
                    if c == qt:
                        bias = [(0, 384, ident_b[:, :], causal3[:, :], [ident_b.b, causal3.b])]
                        qk = [(0, 384, KSW[0:64, 0, c * 128:(c + 1) * 128], q_ap(g, q0), [KSW.b, QN.b])]
                    else:
                        bias = []
                        mq = MQ[c // 32][n % 2]
                        qk = [(0, 384, KSW[:, 0, c * 128:(c + 1) * 128], mq[:, :], [KSW.b, mq.b])]
                    jobs.append(dict(ncol=384, bias=bias, pv=pv, qk=qk))
                jobs[-1]["post"] = post_sel(n, qt, g)

                def pre(o=o):
                    op("dve", lambda e: e.memset(o[:, :, :], 0.0), [], [o.b])
                jobs[0]["pre"] = pre
                return jobs

            order = [(qt, 0) for qt in range(NT)]
            jobs = cmp_jobs(0, *order[0])
            for n, (qt, g) in enumerate(order):
                if n + 1 < len(order):
                    jobs += cmp_jobs(n + 1, *order[n + 1])
                jobs += win_jobs(n, qt, g)
                jobs += sel_jobs(n, qt, g)
            run_jobs(jobs, S, PT)
            cx.barrier(release=[QN.b, KSW.b, VSW.b, szt[0].b, szt[1].b])

    DIL = ((128, 1), (512, 4), (2048, 16))

    def phaseD(L, do_exchange, between=None):
        PAIRLOC = ((0, 0), (0, 1), (1, 0))
        with ExitStack() as ps:
            DQK = TL(cx, ps, "DQK", [128, 4, T], BF16)
            for r in range(4):
                dma("sp", DQK[:, r, :], DQKd.ap()[:, r, :], DQKd.b, DQK.b, waw=False)
            DVp = [TL(cx, ps, "DVp%d" % i, [128, NT, 1, 66], BF16) for i in range(3)]
            PT = [TL(cx, ps, "PTd%d" % i, [128, 512], BF16) for i in range(4)]
            ost = [TL(cx, ps, "ost%d" % i, [128, 1, 65], F32) for i in range(4)]
            S = [TL(cx, ps, "Sd%d" % i, [128, 2, 512], F32, psum=True) for i in range(3)]
            OD = [TL(cx, ps, "OD%d" % i, [128, 1, 65], F32, psum=True) for i in range(2)]
            cnt = [0]
            jobs = []
            for i, (W_, d) in enumerate(DIL):
                Lq = T // d
                nbk = Lq // 128
                if nbk == 0:
                    raise ValueError("sequence too short for dilation")
                cq, hf = PAIRLOC[i]
                hp = slice(hf * 64, (hf + 1) * 64)
                dv = DVp[i]
                for rho in range(d):
                    src = DVd.ap()[rho:T:d, i:i + 1, :].rearrange("(n p) h e -> p n h e", p=128)
                    nsp = max(1, nbk // 16)
                    for k in range(nsp):
                        a, b = k * nbk // nsp, (k + 1) * nbk // nsp
                        dma("sp", dv[:, rho * nbk + a:rho * nbk + b, :, :], src[:, a:b, :, :], DVd.b, dv.b,
                            waw=(rho == 0 and k == 0))
                for rho in range(d):
                    for nb in range(nbk):
                        qs = nb * 128 * d + rho
                        ksl = {1: slice(qs, qs + 127 * d + 1, d)}
                        if nb > 0:
                            ks0 = (nb - 1) * 128 * d + rho
                            ksl[0] = slice(ks0, ks0 + 127 * d + 1, d)
                        n = cnt[0]
                        cnt[0] += 1
                        o = OD[n % 2]
                        pcs = sorted(ksl.keys())
                        qk = [(hf, pc * 128, (pc + 1) * 128, DQK[hp, 2 + cq, ksl[pc]], DQK[hp, cq, ksl[1]], [DQK.b])
                              for pc in pcs]
                        pv = [(o[:, 0, :], pc * 128, (pc + 1) * 128, dv[:, rho * nbk + (nb - 1 + pc), 0, 0:65],
                               pc == pcs[0], pc == pcs[-1], o.b, [dv.b]) for pc in pcs]

                        def post(n=n, o=o, i=i, qs=qs, d=d):
                            st_ = ost[n % 4]
                            op("dve", lambda e: e.tensor_copy(st_[:, :, :], o[:, :, :]), [o.b], [st_.b])
                            dma("pool", DOd.ap()[qs:qs + 127 * d + 1:d, i:i + 1, :], st_[:, :, :], st_.b, DOd.b,
                                waw=False)

                        def pre(o=o):
                            op("dve", lambda e: e.memset(o[:, :, :], 0.0), [], [o.b])
                        jobs.append(dict(qk=qk, pv=pv, post=post, pre=pre, hf=hf))
            run_jobs_d(jobs, S, PT)
            cx.barrier(release=[DQK.b, DVp[0].b, DVp[1].b, DVp[2].b])

        if between is not None:
            between()
        with ExitStack() as ps:
            dot = [TL(cx, ps, "dot%d" % i, [128, 3, 65], F32) for i in range(2)]
            dsum = TL(cx, ps, "dsum", [128, 1], F32)
            yc = TL(cx, ps, "yc", [128, 4, 64], BF16)
            szt = [TL(cx, ps, "sztd%d" % i, [128, 2, 512], BF16) for i in range(2)]
            yst = [TL(cx, ps, "ystd%d" % i, [128, 2, 512], BF16) for i in range(2)]
            TP = TL(cx, ps, "TPd", [128, 4, 128], BF16, psum=True)
            op("pool", lambda e: e.memset(yc[:, :, :], 0.0), [], [yc.b])
            for c in range(NT):
                do = dot[c % 2]
                dma("sp", do[:, :, :], DOd.ap()[c * 128:(c + 1) * 128, :, :], DOd.b, do.b)
                grp = c // 4
                sz = szt[grp % 2]
                ys = yst[grp % 2]
                if c % 4 == 0:
                    dma("sp", sz[:, :, :], SZd.ap()[:, 2:4, grp * 512:(grp + 1) * 512], SZd.b, sz.b)
                op("dve", lambda e, do=do: e.tensor_tensor(dsum[:, :], do[:, 0, 64:65], do[:, 1, 64:65], ADD),
                   [do.b], [dsum.b])
                op("dve", lambda e, do=do: e.tensor_tensor(dsum[:, :], dsum[:, :], do[:, 2, 64:65], ADD),
                   [do.b, dsum.b], [dsum.b])
                op("dve", lambda e: e.reciprocal(dsum[:, :], dsum[:, :]), [dsum.b], [dsum.b])
                op("dve", lambda e, do=do: e.tensor_scalar(
                    yc[:, 0:3, :], do[:, :, 0:64], dsum[:, 0:1], None, op0=MUL),
                    [do.b, dsum.b], [yc.b])
                ycf = yc[:, :, :].rearrange("p i e -> p (i e)")
                qo = (c % 4) * 128
                for k in range(2):
                    op("pe", lambda e, k=k: e.transpose(TP[:, k, :], ycf[:, k * 128:(k + 1) * 128], ident_b[:, :]),
                       [yc.b, ident_b.b], [TP.b])
                for k in range(2):
                    op("dve", lambda e, k=k, qo=qo, ys=ys, sz=sz: e.tensor_tensor(
                        ys[:, k, qo:qo + 128], TP[:, k, :], sz[:, k, qo:qo + 128], MUL),
                        [TP.b, sz.b], [ys.b], waw=False)
                if c % 4 == 3:
                    dma("pool", fm(yTs[L][grp])[:, 3:5, :], ys[:, :, :], ys.b, yTs[L][grp].b, waw=False)
                    if do_exchange:
                        exchange(L, grp)
            cx.barrier(release=[dot[0].b, dot[1].b, szt[0].b, szt[1].b])

    def exchange(L, tt):
        e = cx.engs["pool"]
        src, dst = yTs[L][tt], yTfs[L][tt]
        cx._sync(e, [src.b], [dst.b], True)
        ins = nc.gpsimd.collective_compute(
            "AllGather", ALU.bypass, replica_groups=[[2 * b_, 2 * b_ + 1] for b_ in range(n_pairs)],
            ins=[src.t.ap().opt()], outs=[dst.t.ap().opt()])
        ccn[0] += 1
        ins.then_inc(ccsem, 1)
        tok = ("cc", ccsem, ccn[0])
        dst.b.w["cc"] = tok
        src.b.r["cc"] = tok
        cx.extra_toks["cc"] = tok

    def prepE(L, stack):
        NK = 2 * NYC
        wo = TL(cx, stack, "wo", [128, NK, D_MODEL], BF16)
        wg = TL(cx, stack, "wg", [128, 8, D_MODEL], BF16)
        wp = TL(cx, stack, "wp", [128, 2, D_MODEL], BF16)
        wst = [TL(cx, stack, "wste%d" % i, [128, D_MODEL], F32) for i in range(2)]
        k = 0
        for (src, dst, nk) in ((wout, wo, NK), (wgate, wg, 8), (wproj, wp, 2)):
            for kc in range(nk):
                s = wst[k % 2]
                k += 1
                dma("sp", s[:, :], src.ap()[L, :, kc, :], src.b, s.b)
                op("act", lambda e, dst=dst, kc=kc, s=s: e.copy(dst[:, kc, :], s[:, :]),
                   [s.b], [dst.b], waw=False)
        return wo, wg, wp, wst

    def phaseE(L, last, W):
        src_h = xT if L == 0 else hT
        NK = 2 * NYC
        wo, wg, wp, wst = W
        with ExitStack() as ps:
            yt = [TL(cx, ps, "yt%d" % i, [128, NK, 512], BF16) for i in range(2)]
            htl = [TL(cx, ps, "htl%d" % i, [128, 8, 512], F32) for i in range(2)]
            pts = [TL(cx, ps, "pts%d" % i, [128, 2, 512], F32) for i in range(2)]
            ptb = TL(cx, ps, "ptb", [128, 2, 512], BF16)
            h1b = TL(cx, ps, "h1b", [128, 8, 512], BF16)
            sig = [TL(cx, ps, "sig%d" % i, [128, 512], F32) for i in range(2)]
            sqf = [TL(cx, ps, "sqf%d" % i, [128, 512], F32) for i in range(2)]
            rbc = TL(cx, ps, "rbcE", [128, 512], F32)
            po = [TL(cx, ps, "po%d" % i, [128, 512], F32, psum=True) for i in range(2)]
            pg = [TL(cx, ps, "pg%d" % i, [128, 512], F32, psum=True) for i in range(2)]
            pp = [TL(cx, ps, "pp%d" % i, [128, 512], F32, psum=True) for i in range(2)]
            pst = TL(cx, ps, "pstE", [128, 512], F32, psum=True)
            dst_h = outT if last else hT

            def loads(tt):
                t0 = tt * 512
                dma("sp", yt[tt % 2][:, :, :], fm(yTfs[L][tt])[:, :, :], yTfs[L][tt].b, yt[tt % 2].b)
                dma("sp", htl[tt % 2][:, :, :], fm(src_h)[:, :, t0:t0 + 512], src_h.b, htl[tt % 2].b)
                dma("sp", pts[tt % 2][:, :, :],
                    pT.ap()[L, :, t0:t0 + 512].rearrange("(c p) t -> p c t", p=128), pT.b, pts[tt % 2].b)

            loads(0)
            for tt in range(NTT):
                t0 = tt * 512
                if tt + 1 < NTT:
                    loads(tt + 1)
                y_, h_, p_ = yt[tt % 2], htl[tt % 2], pts[tt % 2]
                op("pool", lambda e, p_=p_: e.tensor_copy(ptb[:, :, :], p_[:, :, :]), [p_.b], [ptb.b])
                for j in range(8):
                    p = po[j % 2]
                    for kc in range(NK):
                        mm(p[:, :], wo[:, kc, j * 128:(j + 1) * 128], y_[:, kc, :], kc == 0, kc == NK - 1,
                           [wo.b, y_.b], [p.b])
                    op("dve", lambda e, j=j, p=p, h_=h_: e.tensor_tensor(h_[:, j, :], p[:, :], h_[:, j, :], ADD),
                       [p.b, h_.b], [h_.b])
                    op("act", lambda e, j=j, h_=h_: e.copy(h1b[:, j, :], h_[:, j, :]), [h_.b], [h1b.b],
                       waw=False)
                for j in range(8):
                    g_ = pg[j % 2]
                    q_ = pp[j % 2]
                    sg = sig[j % 2]
                    for kc in range(8):
                        mm(g_[:, :], wg[:, kc, j * 128:(j + 1) * 128], h1b[:, kc, :], kc == 0, kc == 7,
                           [wg.b, h1b.b], [g_.b])
                    for kc in range(2):
                        mm(q_[:, :], wp[:, kc, j * 128:(j + 1) * 128], ptb[:, kc, :], kc == 0, kc == 1,
                           [wp.b, ptb.b], [q_.b])
                    op("act", lambda e, sg=sg, g_=g_: e.activation(sg[:, :], g_[:, :], AF.Sigmoid), [g_.b], [sg.b])
                    op("dve", lambda e, sg=sg, q_=q_: e.tensor_tensor(sg[:, :], sg[:, :], q_[:, :], MUL),
                       [sg.b, q_.b], [sg.b])
                    op("pool", lambda e, sg=sg, j=j, h_=h_: e.tensor_tensor(h_[:, j, :], h_[:, j, :], sg[:, :], ADD),
                       [sg.b, h_.b], [h_.b])
                if last:
                    for j in range(8):
                        sq = sqf[j % 2]
                        op("act", lambda e, sq=sq, j=j, h_=h_: e.activation(sq[:, :], h_[:, j, :], AF.Square),
                           [h_.b], [sq.b])
                        mm(pst[:, :], ones_f[:, :], sq[:, :], j == 0, j == 7, [ones_f.b, sq.b], [pst.b])
                    op("dve", lambda e: e.tensor_scalar(rbc[:, :], pst[:, :], 1.0 / D_MODEL, EPS, op0=MUL, op1=ADD),
                       [pst.b], [rbc.b])
                    op("act", lambda e: e.activation(rbc[:, :], rbc[:, :], AF.Sqrt), [rbc.b], [rbc.b])
                    op("dve", lambda e: e.reciprocal(rbc[:, :], rbc[:, :]), [rbc.b], [rbc.b])
                    for j in range(8):
                        op("dve", lambda e, j=j, h_=h_: e.scalar_tensor_tensor(
                            h_[:, j, :], h_[:, j, :], gfin_s[:, j:j + 1], rbc[:, :], op0=MUL, op1=MUL),
                            [h_.b, gfin_s.b, rbc.b], [h_.b])
                dma("pool", fm(dst_h)[:, :, t0:t0 + 512], h_[:, :, :], h_.b, dst_h.b, waw=False)
            rel = [b.b for b in yt + htl + pts + wst]
            cx.barrier(release=rel)

    for L in range(depth if nlayers is None else nlayers):
        with ExitStack() as ls:
            G_res = TL(cx, ls, "G_res", [128, NT, 10], F32)
            KCMPT = TL(cx, ls, "KCMPT", [128, NCC * 128], BF16)
            VCX = TL(cx, ls, "VCX", [128, NCC, 2, 65 + NS], BF16)
            with ExitStack() as ab:
                KVC = TL(cx, ab, "KVC", [128, 2, T], BF16)
                if "A" in phases:
                    phaseA(L, G_res, KVC)
                if "B" in phases:
                    phaseB(L, KVC, KCMPT, VCX)
            if "C" in phases:
                phaseC(L, G_res, KCMPT, VCX)
            with ExitStack() as ee:
                W = []
                if "D" in phases:
                    phaseD(L, "X" in phases, (lambda: W.append(prepE(L, ee))) if "E" in phases else None)
                if "E" in phases:
                    if not W:
                        W.append(prepE(L, ee))
                    phaseE(L, L == depth - 1, W[0])
    cx.barrier()
    es.close()
    return nc


def _consts(T):
    NS = T // 64
    NCC = max(1, T // 2048)
    NCMP = T // 16 - 1
    j = np.arange(128)[:, None]
    i = np.arange(128)[None, :]
    f = np.float32
    c = {}
    c["c_ident"] = np.eye(128, dtype=f)
    c["c_causal"] = np.where(j <= i, 0.0, NEG).astype(f)
    c["c_winold"] = np.where(j > i, 0.0, NEG).astype(f)
    c["c_dprev"] = np.where(j >= i, 0.0, NEG).astype(f)
    cm = np.zeros((128, 17, 128), f)
    for di in range(17):
        cm[:, di, :] = np.where(16 * j + 31 <= di * 128 + i, 0.0, NEG)
    c["c_cmpb"] = cm
    we = np.zeros((128, T), f)
    m = np.arange(T)
    we[64 + (m // 64) % 64, m] = 1.0
    c["c_wexp"] = we
    ov = np.zeros((128, NCC, NS), f)
    for cc in range(NCC):
        n = cc * 128 + np.arange(128)[:, None]
        s = np.arange(NS)[None, :]
        o = (16 * n < 64 * s + 64) & (16 * n + 32 > 64 * s) & (n < NCMP)
        ov[:, cc, :] = o
    c["c_ovl"] = ov
    keep = np.zeros((128, 2 * NS), f)
    add = np.zeros((128, 2 * NS), f)
    ii = np.arange(128)
    for xi in range(2 * NS):
        x = xi - NS
        if x <= -2:
            keep[:, xi] = 1.0
        elif x == -1:
            keep[:, xi] = (ii >= 64)
            add[:, xi] = np.where(ii < 64, 1e4, 0.0)
        elif x == 0:
            add[:, xi] = 1e4
        elif x == 1:
            add[:, xi] = np.where(ii >= 64, 1e4, -1.0)
        else:
            add[:, xi] = -1.0
    c["c_keep"] = keep
    c["c_add"] = add
    return c


ZC = 4114


def _col_perm(s_):
    z64 = [ZC] * 64
    r64 = lambda o: list(range(o, o + 64))
    fmc = []
    for off in (0, 256, 512, 768):
        fmc += list(range(off + s_ * 128, off + s_ * 128 + 128))
    for r in range(3):
        fmc += r64(1024 + (s_ * 3 + r) * 64) * 2
    for off in (1408, 1536, 1664, 1920):
        fmc += r64(off + s_ * 64) * 2
    fmc += list(range(2194 + s_ * 192, 2194 + s_ * 192 + 192)) + z64
    heads = [s_, 2 + s_, 4 + s_]
    for off in (2578, 2962, 3730):
        fmc += r64(off + heads[0] * 64) + r64(off + heads[1] * 64) + r64(off + heads[2] * 64) + z64
    tmc = r64(1792 + s_ * 64) + r64(2048 + s_ * 64)
    for h in heads:
        tmc += r64(3346 + h * 64)
    tmc += list(range(2176 + s_ * 9, 2176 + s_ * 9 + 9)) + [ZC]
    assert len(fmc) == NFM * 128 and len(tmc) == NTM, (len(fmc), len(tmc))
    return np.array(fmc + tmc)


def _weights(s_, norm_mix, w_in, conv_w, conv_b, cmp_pe, cmp_w1, cmp_w2, w_out, w_ple_gate, w_ple_proj,
             norm_final):
    depth = w_in.shape[0]
    f = np.float32
    perm = _col_perm(s_)
    d = {}
    wz = np.concatenate([np.asarray(w_in, f), np.zeros((depth, D_MODEL, 1), f)], axis=2)
    d["win"] = np.ascontiguousarray(wz[:, :, perm].reshape(depth, 8, 128, D_IN).transpose(0, 2, 1, 3))
    d["gin"] = np.ascontiguousarray(np.asarray(norm_mix, f).reshape(depth, 8, 128).transpose(0, 2, 1))
    cw = np.zeros((depth, 128, 1, 4), f)
    cw[:, :, 0, 0:3] = np.asarray(conv_w, f).reshape(depth, 3, 2, 128)[:, :, s_, :].transpose(0, 2, 1)
    cw[:, :, 0, 3] = np.asarray(conv_b, f).reshape(depth, 2, 128)[:, s_, :]
    d["convw"] = cw
    w1 = np.asarray(cmp_w1, f).reshape(depth, 2, 32, 64, 64)
    w1b = np.zeros((depth, 2, 2, 64, 32, 2, 64), f)
    for g in range(2):
        w1b[:, :, g, :, :, g, :] = w1.transpose(0, 1, 3, 2, 4)
    d["w1blk"] = w1b.reshape(depth, 2, 128, 32, 128)
    w2 = np.asarray(cmp_w2, f)
    w2b = np.zeros((depth, 2, 2, 64, 2, 64), f)
    for g in range(2):
        w2b[:, :, g, :, g, :] = w2
    d["w2blk"] = w2b.reshape(depth, 2, 128, 128)
    pe = np.asarray(cmp_pe, f).transpose(0, 1, 3, 2)
    d["peT"] = np.ascontiguousarray(np.concatenate([pe, pe], axis=2))
    wo = np.concatenate([np.asarray(w_out, f), np.zeros((depth, 1, D_MODEL), f)], axis=1)
    ZR = D_MODEL
    rows = []
    for q in range(2):
        rows += list(range(q * 128, q * 128 + 128))
        rows += list(range(256 + q * 192, 256 + q * 192 + 192)) + [ZR] * 64
        for h in (q, 2 + q, 4 + q):
            rows += list(range(640 + h * 64, 640 + h * 64 + 64))
        rows += [ZR] * 64
    assert len(rows) == 2 * NYC * 128
    d["wout"] = np.ascontiguousarray(wo[:, rows, :].reshape(depth, 2 * NYC, 128, D_MODEL).transpose(0, 2, 1, 3))
    d["wgate"] = np.ascontiguousarray(np.asarray(w_ple_gate, f).reshape(depth, 8, 128, D_MODEL).transpose(0, 2, 1, 3))
    d["wproj"] = np.ascontiguousarray(np.asarray(w_ple_proj, f).reshape(depth, 2, 128, D_MODEL).transpose(0, 2, 1, 3))
    d["gfin"] = np.ascontiguousarray(np.asarray(norm_final, f).reshape(8, 128).T)
    return d


_NC_CACHE = {}


def run(x, p, weights, debug=False, **bk):
    x = np.asarray(x, np.float32)
    p = np.asarray(p, np.float32)
    B, T, _ = x.shape
    depth = p.shape[0]
    key = (T, depth, debug, tuple(sorted(bk.items())))
    if key not in _NC_CACHE:
        _NC_CACHE[key] = build(T, depth, debug, n_pairs=B, **bk)
    nc = _NC_CACHE[key]
    consts = _consts(T)
    wts = [_weights(s_, **weights) for s_ in range(2)]
    in_maps = []
    for b in range(B):
        xT = np.ascontiguousarray(x[b].T)
        pTb = np.ascontiguousarray(p[:, b].transpose(0, 2, 1))
        for s_ in range(2):
            m = dict(wts[s_])
            m.update(consts)
            m["xT"] = xT
            m["pT"] = pTb
            in_maps.append(m)
    res = run_bass_kernel_spmd(nc, in_maps, core_ids=list(range(2 * B)))
    out = np.stack([np.ascontiguousarray(np.asarray(res.results[2 * b]["outT"]).T) for b in range(B)], axis=0)
    return out.astype(np.float32), res


def kernel(x, p, norm_mix, w_in, conv_w, conv_b, cmp_pe, cmp_w1, cmp_w2, w_out, w_ple_gate, w_ple_proj,
           norm_final):
    weights = dict(norm_mix=norm_mix, w_in=w_in, conv_w=conv_w, conv_b=conv_b, cmp_pe=cmp_pe, cmp_w1=cmp_w1,
                   cmp_w2=cmp_w2, w_out=w_out, w_ple_gate=w_ple_gate, w_ple_proj=w_ple_proj,
                   norm_final=norm_final)
    out, _ = run(x, p, weights)
    return out
```

```python
from contextlib import ExitStack
import numpy as np
import concourse.bass as bass
import concourse.mybir as mybir
from concourse.bass_utils import run_bass_kernel_spmd

F32 = mybir.dt.float32
BF16 = mybir.dt.bfloat16
AF = mybir.ActivationFunctionType
ALU = mybir.AluOpType

D_MODEL = 1024
PLE_DIM = 256
EPS = 1e-6
NEG = -30000.0
SCALE = 0.125
NFM = 19
NTM = 330
D_IN = NFM * 128 + NTM
D_INP = (D_IN + 15) // 16 * 16
NYC = 5


class Buf:
    def __init__(self, name):
        self.name = name
        self.w = {}
        self.r = {}
        self.slots = None
        self.dn = 0
        self.dq = None


class Eng:
    def __init__(self, key, eng, sem, selfsync):
        self.key, self.eng, self.sem, self.selfsync = key, eng, sem, selfsync
        self.n = 0
        self.waited = {}


class Ctx:
    def __init__(self, nc, es):
        self.nc = nc
        self.es = es
        self.engs = {}
        for key, eng, ss in (("pe", nc.tensor, False), ("act", nc.scalar, True),
                             ("dve", nc.vector, True), ("pool", nc.gpsimd, True),
                             ("sp", nc.sync, False)):
            sem = es.enter_context(nc.semaphore("s_" + key))
            self.engs[key] = Eng(key, eng, sem, ss)
        self.free_slots = []
        self.extra_toks = {}
        self.n_dsems = 0
        self.live_dma_bufs = []

    def buf(self, name):
        return Buf(name)

    def _new_slot(self):
        self.n_dsems += 1
        key = "d%d" % self.n_dsems
        return [key, self.es.enter_context(self.nc.semaphore(key)), 0]

    def _dsem(self, b, qkey):
        if b.slots is None:
            b.dq = qkey
            if qkey == "sp":
                b.slots = [self.free_slots.pop() if self.free_slots else self._new_slot() for _ in range(2)]
            else:
                b.slots = [self._new_slot() for _ in range(3)]
            self.live_dma_bufs.append(b)

    def _wait(self, e, tok):
        k, sem, v = tok
        if k == e.key and not e.selfsync:
            return
        if e.waited.get(k, 0) >= v:
            return
        e.eng.wait_ge(sem, v)
        e.waited[k] = v

    def _sync(self, e, reads, writes, waw):
        deps = {}

        def add(tok):
            if deps.get(tok[0], (None, None, 0))[2] < tok[2]:
                deps[tok[0]] = tok
        for b in reads:
            for t in b.w.values():
                add(t)
        for b in writes:
            if waw:
                for t in b.w.values():
                    add(t)
            for t in b.r.values():
                add(t)
        for tok in deps.values():
            self._wait(e, tok)

    def op(self, ekey, fn, reads=(), writes=(), waw=True):
        e = self.engs[ekey]
        self._sync(e, reads, writes, waw)
        ins = fn(e.eng)
        e.n += 1
        ins.then_inc(e.sem, 1)
        tok = (e.key, e.sem, e.n)
        for b in reads:
            b.r[e.key] = tok
        for b in writes:
            b.w[e.key] = tok
        return ins

    def dma(self, qkey, out, in_, src, dst, waw=True):
        e = self.engs[qkey]
        self._sync(e, [src] if src is not None else [], [dst], waw)
        self._dsem(dst, qkey)
        assert dst.dq == qkey
        slot = dst.slots[dst.dn % len(dst.slots)]
        dst.dn += 1
        if slot[2] > 0:
            self._wait(e, (slot[0], slot[1], 16 * slot[2]))
        slot[2] += 1
        e.eng.dma_start(out=out, in_=in_).then_inc(slot[1], 16)
        tok = (slot[0], slot[1], 16 * slot[2])
        dst.w[slot[0]] = tok
        if src is not None:
            src.r[slot[0]] = tok

    def barrier(self, release=()):
        toks = [(e.key, e.sem, e.n) for e in self.engs.values() if e.n > 0]
        toks += list(self.extra_toks.values())
        for b in self.live_dma_bufs:
            toks += [(s_[0], s_[1], 16 * s_[2]) for s_ in b.slots if s_[2] > 0]
        for e in self.engs.values():
            for tok in toks:
                if tok[0] == e.key:
                    continue
                self._wait(e, tok)
        for b in release:
            if b.slots is not None and b.dq == "sp":
                self.free_slots += b.slots
                self.live_dma_bufs.remove(b)
                b.slots = None


class TL:
    uid = 0

    def __init__(self, ctx, stack, name, shape, dtype, psum=False):
        TL.uid += 1
        nm = "%s_%d" % (name, TL.uid)
        if psum:
            esz = 4 if dtype == F32 else 2
            n = 1
            for d_ in shape[1:]:
                n *= d_
            assert shape[0] == 128
            if n * esz > 2048:
                assert n * esz == 4096 and len(shape) == 3 and shape[1] == 2
                full = stack.enter_context(ctx.nc.psum_tensor(nm, [128, 4096 // esz], dtype))
            else:
                full = stack.enter_context(ctx.nc.psum_tensor(nm, [128, 2048 // esz], dtype))
            v = full[:, 0:n]
            if len(shape) == 3:
                v = v.rearrange("p (a b) -> p a b", a=shape[1])
            elif len(shape) == 4:
                v = v.rearrange("p (a b c) -> p a b c", a=shape[1], b=shape[2])
            self.t = v
        else:
            self.t = stack.enter_context(ctx.nc.sbuf_tensor(nm, list(shape), dtype))
        self.b = ctx.buf(name)

    def __getitem__(self, idx):
        return self.t[idx]


class DT:
    def __init__(self, ctx, name, shape, dtype, kind=None):
        if kind is None:
            self.t = ctx.nc.dram_tensor(name, list(shape), dtype)
        else:
            self.t = ctx.nc.dram_tensor(name, list(shape), dtype, kind=kind)
        self.b = ctx.buf(name)

    def ap(self):
        return self.t.ap()


def build(T, depth, debug=False, phases="ABCDXE", nlayers=None, n_pairs=4):
    nc = bass.Bass("TRN2", target_bir_lowering=False)
    NT = T // 128
    NTT = T // 512
    NS = T // 64
    NCC = max(1, T // 2048)
    NCMP = T // 16 - 1
    es = ExitStack()
    cx = Ctx(nc, es)
    op, dma = cx.op, cx.dma
    skind = "ExternalOutput" if debug else None

    def ext_in(name, shape):
        return DT(cx, name, shape, F32, kind="ExternalInput")

    xT = ext_in("xT", [D_MODEL, T])
    pT = ext_in("pT", [depth, PLE_DIM, T])
    win = ext_in("win", [depth, 128, 8, D_IN])
    gin = ext_in("gin", [depth, 128, 8])
    convw = ext_in("convw", [depth, 128, 1, 4])
    w1blk = ext_in("w1blk", [depth, 2, 128, 32, 128])
    w2blk = ext_in("w2blk", [depth, 2, 128, 128])
    peT = ext_in("peT", [depth, 2, 128, 32])
    wout = ext_in("wout", [depth, 128, 2 * NYC, D_MODEL])
    wgate = ext_in("wgate", [depth, 128, 8, D_MODEL])
    wproj = ext_in("wproj", [depth, 128, 2, D_MODEL])
    gfin = ext_in("gfin", [128, 8])
    c_ident = ext_in("c_ident", [128, 128])
    c_causal = ext_in("c_causal", [128, 128])
    c_winold = ext_in("c_winold", [128, 128])
    c_dprev = ext_in("c_dprev", [128, 128])
    c_cmpb = ext_in("c_cmpb", [128, 17, 128])
    c_wexp = ext_in("c_wexp", [128, T])
    c_ovl = ext_in("c_ovl", [128, NCC, NS])
    c_keep = ext_in("c_keep", [128, 2 * NS])
    c_add = ext_in("c_add", [128, 2 * NS])
    outT = DT(cx, "outT", [D_MODEL, T], F32, kind="ExternalOutput")

    hT = DT(cx, "hT", [D_MODEL, T], F32, kind=skind)
    yTs = [[DT(cx, "yT%d_%d" % (l, t_), [NYC * 128, 512], BF16) for t_ in range(NTT)] for l in range(depth)]
    yTfs = [[DT(cx, "yTf%d_%d" % (l, t_), [2 * NYC * 128, 512], BF16) for t_ in range(NTT)]
            for l in range(depth)]
    for l in range(depth):
        for t_ in range(1, NTT):
            yTs[l][t_].b = yTs[l][0].b
    QNd = DT(cx, "QNd", [128, 3, T], BF16, kind=skind)
    KSWd = DT(cx, "KSWd", [128, 2, T], BF16, kind=skind)
    SZd = DT(cx, "SZd", [128, 4, T], BF16, kind=skind)
    DQKd = DT(cx, "DQKd", [128, 4, T], BF16, kind=skind)
    VSWd = DT(cx, "VSWd", [T, 2, 66], BF16, kind=skind)
    DVd = DT(cx, "DVd", [T, 3, 66], BF16, kind=skind)
    DOd = DT(cx, "DOd", [T, 3, 65], F32, kind=skind)
    ccsem = es.enter_context(nc.semaphore("ccsem"))
    ccn = [0]

    def fm(d):
        return d.ap().rearrange("(c p) t -> p c t", p=128)

    gs = es
    ident_b = TL(cx, gs, "ident_b", [128, 128], BF16)
    causal3 = TL(cx, gs, "causal3", [128, 384], BF16)
    winold3 = TL(cx, gs, "winold3", [128, 384], BF16)
    dilb = TL(cx, gs, "dilb", [128, 512], BF16)
    ones_f = TL(cx, gs, "ones_f", [128, 128], F32)
    keepW = TL(cx, gs, "keepW", [128, 2 * NS], F32)
    addW = TL(cx, gs, "addW", [128, 2 * NS], F32)
    gfin_s = TL(cx, gs, "gfin_s", [128, 8], F32)
    eps_t = TL(cx, gs, "eps_t", [128, 1], F32)
    e0f = TL(cx, gs, "e0f", [128, 1], F32)

    with ExitStack() as ss:
        st = [TL(cx, ss, "cst%d" % i, [128, 128], F32) for i in range(4)]
        for i, src in enumerate((c_ident, c_causal, c_winold, c_dprev)):
            dma("sp", st[i][:, :], src.ap()[:, :], src.b, st[i].b)
        dma("sp", keepW[:, :], c_keep.ap()[:, :], c_keep.b, keepW.b)
        dma("sp", addW[:, :], c_add.ap()[:, :], c_add.b, addW.b)
        dma("sp", gfin_s[:, :], gfin.ap()[:, :], gfin.b, gfin_s.b)
        op("dve", lambda e: e.tensor_copy(ident_b[:, :], st[0][:, :]), [st[0].b], [ident_b.b])
        op("dve", lambda e: e.tensor_copy(e0f[:, :], st[0][:, 0:1]), [st[0].b], [e0f.b])
        for r in range(3):
            op("dve", lambda e, r=r: e.tensor_copy(causal3[:, r * 128:(r + 1) * 128], st[1][:, :]),
               [st[1].b], [causal3.b])
            op("dve", lambda e, r=r: e.tensor_copy(winold3[:, r * 128:(r + 1) * 128], st[2][:, :]),
               [st[2].b], [winold3.b])
        for k in range(4):
            s = st[3] if k % 2 == 0 else st[1]
            op("dve", lambda e, k=k, s=s: e.tensor_copy(dilb[:, k * 128:(k + 1) * 128], s[:, :]),
               [s.b], [dilb.b])
        op("pool", lambda e: e.memset(ones_f[:, :], 1.0), [], [ones_f.b])
        op("pool", lambda e: e.memset(eps_t[:, :], EPS), [], [eps_t.b])
        cx.barrier(release=[s.b for s in st])

    MUL, ADD = ALU.mult, ALU.add

    def mm(out, lhsT, rhs, start, stop, reads, writes, skip=False):
        op("pe", lambda e: e.matmul(out, lhsT, rhs, start=start, stop=stop, skip_group_check=skip), reads, writes)

    def phaseA(L, G_res, KVC):
        src_h = xT if L == 0 else hT
        with ExitStack() as ps:
            wb = TL(cx, ps, "wb", [128, 8, D_INP], BF16)
            gsb = TL(cx, ps, "gsb", [128, 8], F32)
            cw = TL(cx, ps, "cw", [128, 1, 4], F32)
            dma("sp", gsb[:, :], gin.ap()[L, :, :], gin.b, gsb.b)
            dma("sp", cw[:, :, :], convw.ap()[L, :, :, :], convw.b, cw.b)
            with ExitStack() as ws:
                wst = [TL(cx, ws, "wst%d" % i, [128, D_IN], F32) for i in range(2)]
                for kc in range(8):
                    s = wst[kc % 2]
                    dma("sp", s[:, :], win.ap()[L, :, kc, :], win.b, s.b)
                    op("dve", lambda e, kc=kc, s=s: e.tensor_scalar(
                        wb[:, kc, 0:1856], s[:, 0:1856], gsb[:, kc:kc + 1], None, op0=MUL),
                        [s.b, gsb.b], [wb.b], waw=False)
                    op("act", lambda e, kc=kc, s=s: e.activation(
                        wb[:, kc, 1856:D_IN], s[:, 1856:D_IN], AF.Identity, scale=gsb[:, kc:kc + 1]),
                        [s.b, gsb.b], [wb.b], waw=False)
                cx.barrier(release=[s.b for s in wst])

            ht = TL(cx, ps, "ht", [128, 8, 512], F32)
            sqc = [TL(cx, ps, "sqc%d" % i, [128, 512], F32) for i in range(2)]
            xb = [TL(cx, ps, "xb%d" % i, [128, 8, 512], BF16) for i in range(2)]
            rbc = TL(cx, ps, "rbc", [128, 512], F32)
            rtm = TL(cx, ps, "rtm", [128, 4], F32)
            tb = TL(cx, ps, "tb", [128, 512], F32)
            t1 = TL(cx, ps, "t1", [128, 512], F32)
            th = TL(cx, ps, "th", [128, 512], F32)
            tz = TL(cx, ps, "tz", [128, 512], F32)
            tzz = TL(cx, ps, "tzz", [128, 512], F32)
            ch = [TL(cx, ps, "ch%d" % i, [128, 514], F32) for i in range(1)]
            sQ = TL(cx, ps, "sQ", [128, 3, 512], BF16)
            sK = TL(cx, ps, "sK", [128, 2, 512], BF16)
            sZ = TL(cx, ps, "sZ", [128, 4, 512], BF16)
            sD = TL(cx, ps, "sD", [128, 4, 512], BF16)
            sY = TL(cx, ps, "sY", [128, 1, 512], BF16)
            gtmp = TL(cx, ps, "gtmp", [128, 4, 10], F32)
            sV = TL(cx, ps, "sV", [128, 4, 2, 66], BF16)
            sDV = TL(cx, ps, "sDV", [128, 4, 3, 66], BF16)
            pa = [TL(cx, ps, "pa%d" % i, [128, 512], F32, psum=True) for i in range(3)]
            ptm0 = TL(cx, ps, "ptm0", [128, 512], F32, psum=True)
            ptm1 = TL(cx, ps, "ptm1", [128, 512], F32, psum=True)
            pst = TL(cx, ps, "pst", [128, 512], F32, psum=True)
            pst2 = TL(cx, ps, "pst2", [128, 4], F32, psum=True)

            for c in ch:
                op("pool", lambda e, c=c: e.memset(c[:, :], 0.0), [], [c.b])
            op("pool", lambda e: e.memset(sV[:, :, :, :], 1.0), [], [sV.b])
            op("pool", lambda e: e.memset(sDV[:, :, :, :], 1.0), [], [sDV.b])

            pai = [0]

            def proj_chunk(j, xbt):
                p = pa[pai[0] % 3]
                pai[0] += 1
                for kc in range(8):
                    mm(p[:, :], wb[:, kc, j * 128:(j + 1) * 128], xbt[:, kc, :], kc == 0, kc == 7,
                       [wb.b, xbt.b], [p.b])
                return p

            def evac_mul(p, dst_ap, dst_b):
                op("dve", lambda e: e.tensor_tensor(dst_ap, p[:, :], rbc[:, :], MUL),
                   [p.b, rbc.b], [dst_b], waw=False)

            def evac_silu(p, dst_ap, dst_b):
                op("dve", lambda e: e.tensor_tensor(tzz[:, :], p[:, :], rbc[:, :], MUL),
                   [p.b, rbc.b], [tzz.b])
                op("act", lambda e: e.activation(dst_ap, tzz[:, :], AF.Silu), [tzz.b], [dst_b], waw=False)

            import os as _osA
            _astep = int(_osA.environ.get("KDBG_A", "9"))
            for tt in range(NTT if _astep >= 1 else 0):
                t0 = tt * 512
                xbt = xb[tt % 2]
                dma("sp", ht[:, :, :], fm(src_h)[:, :, t0:t0 + 512], src_h.b, ht.b)
                for kc in range(8):
                    sq = sqc[kc % 2]
                    op("act", lambda e, kc=kc, sq=sq: e.activation(sq[:, :], ht[:, kc, :], AF.Square),
                       [ht.b], [sq.b])
                    mm(pst[:, :], ones_f[:, :], sq[:, :], kc == 0, kc == 7, [ones_f.b, sq.b], [pst.b])
                op("act", lambda e, xbt=xbt: e.copy(xbt[:, :, :], ht[:, :, :]), [ht.b], [xbt.b])
                op("dve", lambda e: e.tensor_scalar(
                    rbc[:, :], pst[:, :], 1.0 / D_MODEL, EPS, op0=MUL, op1=ADD), [pst.b], [rbc.b])
                op("act", lambda e: e.activation(rbc[:, :], rbc[:, :], AF.Sqrt), [rbc.b], [rbc.b])
                op("dve", lambda e: e.reciprocal(rbc[:, :], rbc[:, :]), [rbc.b], [rbc.b])
                for sub in range(4):
                    mm(pst2[:, sub:sub + 1], rbc[:, sub * 128:(sub + 1) * 128], e0f[:, 0:1],
                       True, True, [rbc.b, e0f.b], [pst2.b])
                op("dve", lambda e: e.tensor_copy(rtm[:, :], pst2[:, :]), [pst2.b], [rtm.b])

                if _astep < 2:
                    continue
                for cc in range(1):
                    p = proj_chunk(0, xbt)
                    evac_mul(p, tb[:, :], tb.b)
                    p = proj_chunk(1, xbt)
                    evac_mul(p, t1[:, :], t1.b)
                    p = proj_chunk(2, xbt)
                    evac_mul(p, th[:, :], th.b)
                    p = proj_chunk(3, xbt)
                    evac_mul(p, tz[:, :], tz.b)
                    op("act", lambda e: e.activation(tz[:, :], tz[:, :], AF.Silu), [tz.b], [tz.b])
                    c = ch[cc]
                    op("pool", lambda e, c=c: e.tensor_tensor(c[:, 2:514], t1[:, :], th[:, :], MUL),
                       [t1.b, th.b], [c.b])
                    op("dve", lambda e, c=c, cc=cc: e.tensor_scalar(
                        t1[:, :], c[:, 2:514], cw[:, cc, 2:3], cw[:, cc, 3:4], op0=MUL, op1=ADD),
                        [c.b, cw.b], [t1.b])
                    op("dve", lambda e, c=c, cc=cc: e.scalar_tensor_tensor(
                        t1[:, :], c[:, 1:513], cw[:, cc, 1:2], t1[:, :], op0=MUL, op1=ADD),
                        [c.b, cw.b, t1.b], [t1.b])
                    op("dve", lambda e, c=c, cc=cc: e.scalar_tensor_tensor(
                        t1[:, :], c[:, 0:512], cw[:, cc, 0:1], t1[:, :], op0=MUL, op1=ADD),
                        [c.b, cw.b, t1.b], [t1.b])
                    op("pool", lambda e: e.tensor_tensor(t1[:, :], t1[:, :], tb[:, :], MUL),
                       [t1.b, tb.b], [t1.b])
                    op("pool", lambda e, cc=cc: e.tensor_tensor(sY[:, cc, :], t1[:, :], tz[:, :], MUL),
                       [t1.b, tz.b], [sY.b])
                    op("pool", lambda e, c=c: e.tensor_copy(c[:, 0:2], c[:, 512:514]), [c.b], [c.b])
                dma("pool", fm(yTs[L][tt])[:, 0:1, :], sY[:, :, :], sY.b, yTs[L][tt].b, waw=False)

                if _astep < 3:
                    continue
                for j in (4, 5, 6):
                    evac_mul(proj_chunk(j, xbt), sQ[:, j - 4, :], sQ.b)
                dma("pool", QNd.ap()[:, :, t0:t0 + 512], sQ[:, :, :], sQ.b, QNd.b, waw=False)
                for j in (7, 8):
                    evac_mul(proj_chunk(j, xbt), KVC[:, j - 7, t0:t0 + 512], KVC.b)
                for j in (9, 10):
                    evac_mul(proj_chunk(j, xbt), sK[:, j - 9, :], sK.b)
                dma("pool", KSWd.ap()[:, :, t0:t0 + 512], sK[:, :, :], sK.b, KSWd.b, waw=False)
                for j in (11, 12, 17, 18):
                    zi = j - 11 if j < 13 else 2 + j - 17
                    evac_silu(proj_chunk(j, xbt), sZ[:, zi, :], sZ.b)
                dma("pool", SZd.ap()[:, :, t0:t0 + 512], sZ[:, :, :], sZ.b, SZd.b, waw=False)
                for j in range(13, 17):
                    evac_mul(proj_chunk(j, xbt), sD[:, j - 13, :], sD.b)
                dma("pool", DQKd.ap()[:, :, t0:t0 + 512], sD[:, :, :], sD.b, DQKd.b, waw=False)

                if _astep < 4:
                    continue
                for sub in range(4):
                    for (pt, c0, c1) in ((ptm0, 0, NTM),):
                        for kc in range(8):
                            mm(pt[:, 0:c1 - c0], xbt[:, kc, sub * 128:(sub + 1) * 128],
                               wb[:, kc, NFM * 128 + c0:NFM * 128 + c1], kc == 0, kc == 7,
                               [wb.b, xbt.b], [pt.b])
                    rs = rtm[:, sub:sub + 1]
                    if _astep < 5:
                        continue
                    op("dve", lambda e, sub=sub, rs=rs: e.tensor_scalar(
                        sV[:, sub, :, 0:64], ptm0[:, 0:128].rearrange("p (a e) -> p a e", e=64),
                        rs, None, op0=MUL), [ptm0.b, rtm.b], [sV.b], waw=False)
                    op("dve", lambda e, sub=sub, rs=rs: e.tensor_scalar(
                        sDV[:, sub, :, 0:64], ptm0[:, 128:320].rearrange("p (a e) -> p a e", e=64),
                        rs, None, op0=MUL), [ptm0.b, rtm.b], [sDV.b], waw=False)
                    if _astep < 6:
                        continue
                    op("dve", lambda e, sub=sub, rs=rs: e.tensor_scalar(
                        gtmp[:, sub, :], ptm0[:, 320:330], rs, None, op0=MUL), [ptm0.b, rtm.b], [gtmp.b],
                        waw=False)
                    op("act", lambda e, sub=sub, tt=tt: e.activation(
                        G_res[:, tt * 4 + sub, :], gtmp[:, sub, :], AF.Sigmoid),
                        [gtmp.b], [G_res.b], waw=False)
                if _astep < 7:
                    continue
                dma("pool", VSWd.ap()[t0:t0 + 512, :, :].rearrange("(s p) b e -> p s b e", p=128),
                    sV[:, :, :, :], sV.b, VSWd.b, waw=False)
                dma("pool", DVd.ap()[t0:t0 + 512, :, :].rearrange("(s p) b e -> p s b e", p=128),
                    sDV[:, :, :, :], sDV.b, DVd.b, waw=False)
            cx.barrier(release=[ht.b, sQ.b, sK.b, sZ.b, sD.b, sY.b, sV.b, sDV.b, wb.b, gsb.b, cw.b])

    def phaseB(L, KVC, KCMPT, VCX):
        with ExitStack() as ps:
            w1s = TL(cx, ps, "w1s", [128, 32, 128], F32)
            w1b = [TL(cx, ps, "w1b%d" % k, [128, 32, 128], BF16) for k in range(2)]
            w2s = TL(cx, ps, "w2s", [128, 2, 128], F32)
            w2b = TL(cx, ps, "w2b", [128, 2, 128], BF16)
            pes = TL(cx, ps, "pes", [128, 2, 32], F32)
            peb = TL(cx, ps, "peb", [128, 2, 32], BF16)
            cb = TL(cx, ps, "cb", [128, 2], F32)
            hid = TL(cx, ps, "hid", [128, NCC * 128], BF16)
            ovs = TL(cx, ps, "ovs", [128, NCC, NS], F32)
            pcm = TL(cx, ps, "pcm", [128, 512], F32, psum=True)
            pcb = TL(cx, ps, "pcb", [128, 2], F32, psum=True)
            pk = TL(cx, ps, "pk", [128, 512], F32, psum=True)
            pv = TL(cx, ps, "pv", [128, 128], F32, psum=True)
            for kv in range(2):
                dma("sp", w1s[:, :, :], w1blk.ap()[L, kv, :, :, :], w1blk.b, w1s.b)
                op("dve", lambda e, kv=kv: e.tensor_copy(w1b[kv][:, :, :], w1s[:, :, :]), [w1s.b], [w1b[kv].b])
            dma("sp", w2s[:, :, :], w2blk.ap()[L, :, :, :].rearrange("k p f -> p k f"), w2blk.b, w2s.b)
            op("dve", lambda e: e.tensor_copy(w2b[:, :, :], w2s[:, :, :]), [w2s.b], [w2b.b])
            dma("sp", pes[:, :, :], peT.ap()[L, :, :, :].rearrange("k p l -> p k l"), peT.b, pes.b)
            op("dve", lambda e: e.tensor_copy(peb[:, :, :], pes[:, :, :]), [pes.b], [peb.b])
            op("pool", lambda e: e.memset(hid[:, :], 0.0), [], [hid.b])
            dma("sp", ovs[:, :, :], c_ovl.ap()[:, :, :], c_ovl.b, ovs.b)
            op("pool", lambda e: e.memset(VCX[:, :, :, :], 1.0), [], [VCX.b])
            for g in range(2):
                op("dve", lambda e, g=g: e.tensor_copy(VCX[:, :, g, 65:65 + NS], ovs[:, :, :]),
                   [ovs.b], [VCX.b])
            for kv in range(2):
                for l in range(32):
                    mm(pcb[:, kv:kv + 1], w1b[kv][:, l, :], peb[:, kv, l:l + 1], l == 0, l == 31,
                       [w1b[kv].b, peb.b], [pcb.b])
            op("dve", lambda e: e.tensor_copy(cb[:, :], pcb[:, :]), [pcb.b], [cb.b])
            for kv in range(2):
                for l in range(32):
                    mm(pcm[:, 0:NCMP], w1b[kv][:, l, :], KVC[:, kv, l:l + 16 * (NCMP - 1) + 1:16],
                       l == 0, l == 31, [w1b[kv].b, KVC.b], [pcm.b])
                op("act", lambda e, kv=kv: e.activation(hid[:, 0:NCMP], pcm[:, 0:NCMP], AF.Silu,
                                                        bias=cb[:, kv:kv + 1]),
                   [pcm.b, cb.b], [hid.b])
                if kv == 0:
                    mm(pk[:, 0:NCC * 128], w2b[:, 0, :], hid[:, :], True, True, [w2b.b, hid.b], [pk.b])
                    op("dve", lambda e: e.tensor_copy(KCMPT[:, :], pk[:, 0:NCC * 128]), [pk.b], [KCMPT.b])
                else:
                    for c in range(NCC):
                        mm(pv[:, :], hid[:, c * 128:(c + 1) * 128], w2b[:, 1, :], True, True,
                           [w2b.b, hid.b], [pv.b])
                        op("dve", lambda e, c=c: e.tensor_copy(
                            VCX[:, c, :, 0:64], pv[:, :].rearrange("p (g e) -> p g e", e=64)),
                            [pv.b], [VCX.b])
            cx.barrier(release=[w1s.b, w2s.b, pes.b, ovs.b])

    def run_jobs(jobs, S, PT, depth_=2):
        n = len(jobs)
        nS, nP = len(S), len(PT)
        assert nS > depth_ and nP > depth_
        for i in range(n + depth_):
            if i < n:
                j = jobs[i]
                s = S[i % nS]
                bias = j.get("bias") or []
                for (c0, c1, kT, q, rd) in j["qk"]:
                    mm(s[:, c0:c1], kT, q, True, not bias, rd, [s.b])
                for bi, (c0, c1, bl, br, brd) in enumerate(bias):
                    mm(s[:, c0:c1], bl, br, False, bi == len(bias) - 1, brd, [s.b])
            k = i - depth_
            if k >= 0:
                j = jobs[k]
                s = S[k % nS]
                pt = PT[k % nP]
                ncol = j["ncol"]
                op("act", lambda e, s=s, pt=pt, ncol=ncol: e.activation(
                    pt[:, 0:ncol], s[:, 0:ncol], AF.Exp, scale=SCALE), [s.b], [pt.b])
                for (out_ap, a, b, rhs_ap, st_, sp_, obuf, rbufs) in j["pv"]:
                    mm(out_ap, pt[:, a:b], rhs_ap, False, sp_, [pt.b] + rbufs, [obuf], skip=True)
                if j.get("post"):
                    j["post"]()
            k = i - depth_ + 1
            if 0 <= k < n and jobs[k].get("pre"):
                jobs[k]["pre"]()

    def run_jobs_d(jobs, S2, PT, depth_=2):
        n = len(jobs)
        nS, nP = len(S2), len(PT)
        assert nS > depth_ and nP > depth_
        for i in range(n + depth_):
            if i < n:
                j = jobs[i]
                s = S2[i % nS]
                hf = j["hf"]
                for qi, (h, c0, c1, kT, q, rd) in enumerate(j["qk"]):
                    mm(s[:, h, c0:c1], kT, q, qi == 0, False, rd, [s.b], skip=True)
                mm(s[:, hf, 0:256], ident_b[:, :], dilb[:, 0:256], False, True, [ident_b.b, dilb.b], [s.b],
                   skip=True)
            k = i - depth_
            if k >= 0:
                j = jobs[k]
                s = S2[k % nS]
                pt = PT[k % nP]
                hf = j["hf"]
                op("act", lambda e, s=s, pt=pt, hf=hf: e.activation(
                    pt[:, 0:256], s[:, hf, 0:256], AF.Exp, scale=SCALE), [s.b], [pt.b])
                for (out_ap, a, b, rhs_ap, st_, sp_, obuf, rbufs) in j["pv"]:
                    mm(out_ap, pt[:, a:b], rhs_ap, False, sp_, [pt.b] + rbufs, [obuf], skip=True)
                if j.get("post"):
                    j["post"]()
            k = i - depth_ + 1
            if 0 <= k < n and jobs[k].get("pre"):
                jobs[k]["pre"]()

    def phaseC(L, G_res, KCMPT, VCX):
        with ExitStack() as ps:
            QN = TL(cx, ps, "QN", [128, 3, T], BF16)
            KSW = TL(cx, ps, "KSW", [128, 2, T], BF16)
            VSW = TL(cx, ps, "VSW", [128, NT, 2, 66], BF16)
            cmpb = TL(cx, ps, "cmpb", [128, 17, 128], BF16)
            with ExitStack() as ss:
                wes = TL(cx, ss, "wes", [128, 2048], F32)
                cms = TL(cx, ss, "cms", [128, 17, 128], F32)
                dma("sp", cms[:, :, :], c_cmpb.ap()[:, :, :], c_cmpb.b, cms.b)
                op("dve", lambda e: e.tensor_copy(cmpb[:, :, :], cms[:, :, :]), [cms.b], [cmpb.b])
                for r in range(3):
                    dma("sp", QN[:, r, :], QNd.ap()[:, r, :], QNd.b, QN.b, waw=False)
                for r in range(2):
                    dma("sp", KSW[:, r, :], KSWd.ap()[:, r, :], KSWd.b, KSW.b, waw=False)
                for k in range(T // 2048):
                    dma("sp", wes[:, :], c_wexp.ap()[:, k * 2048:(k + 1) * 2048], c_wexp.b, wes.b)
                    op("dve", lambda e, k=k: e.tensor_copy(KSW[64:128, 0, k * 2048:(k + 1) * 2048], wes[64:128, :]),
                       [wes.b], [KSW.b])
                vsrc = VSWd.ap().rearrange("(c p) b e -> p c b e", p=128)
                nsp = max(1, NT // 16)
                for k in range(nsp):
                    c0, c1 = k * NT // nsp, (k + 1) * NT // nsp
                    dma("sp", VSW[:, c0:c1, :, :], vsrc[:, c0:c1, :, :], VSWd.b, VSW.b, waw=False)
                cx.barrier(release=[wes.b, cms.b])

            PT = [TL(cx, ps, "PT%d" % i, [128, 512], BF16) for i in range(4)]
            NWIN = max(1, NS // 64)
            MQ = [[TL(cx, ps, "MQ%d_%d" % (w, i), [128, 384], BF16) for i in range(2)] for w in range(NWIN)]
            negq2 = TL(cx, ps, "negq2", [128, 128], BF16)
            op("pool", lambda e: e.memset(negq2[:, :], 0.0), [], [negq2.b])
            rd = [TL(cx, ps, "rd%d" % i, [128, 3, 3], F32) for i in range(2)]
            coef = TL(cx, ps, "coef", [128, 3, 3], F32)
            impf = TL(cx, ps, "impf", [128, NS], F32)
            imp2 = TL(cx, ps, "imp2", [128, NS], F32)
            m8 = TL(cx, ps, "m8", [128, 16], F32)
            negq = TL(cx, ps, "negq", [128, 128], BF16)
            accf = TL(cx, ps, "accf", [128, 64], F32)
            NSAO = [TL(cx, ps, "NSAO%d" % i, [128, 1, 4, 64], BF16) for i in range(2)]
            szt = [TL(cx, ps, "szt%d" % i, [128, 2, 512], BF16) for i in range(2)]
            yst = [TL(cx, ps, "yst%d" % i, [128, 2, 512], BF16) for i in range(2)]
            for t_ in NSAO:
                op("pool", lambda e, t_=t_: e.memset(t_[:, :, :, :], 0.0), [], [t_.b])
            S = [TL(cx, ps, "S%d" % i, [128, 512], F32, psum=True) for i in range(3)]
            IMP = TL(cx, ps, "IMP", [128, 3, 128], F32, psum=True)
            OCW = [TL(cx, ps, "OCW%d" % i, [128, 2, 3, 65], F32, psum=True) for i in range(2)]
            OS = [TL(cx, ps, "OS%d" % i, [128, 3, 65], F32, psum=True) for i in range(1)] * 2
            TP = TL(cx, ps, "TP", [128, 4, 128], BF16, psum=True)
            tpb = [TP.b] * 4
            if NS < 128:
                op("pool", lambda e: e.memset(negq[:, :], 0.0), [], [negq.b])

            def q_ap(g, q0):
                return QN[g * 64:(g + 1) * 64, :, q0:q0 + 128]

            def pv_list(out_tl, out_fn, vfn, c, first, last):
                return [(out_fn(r), r * 128, (r + 1) * 128, vfn(c), first, last, out_tl.b, [VSW.b, VCX.b])
                        for r in range(3)]

            def post_cmp(n, qt, g):
                def f():
                    o = OCW[n % 2]
                    r_ = rd[n % 2]
                    op("dve", lambda e: e.tensor_scalar(r_[:, 0, :], o[:, 0, :, 64], 1e-30, None, op0=ALU.max),
                       [o.b], [r_.b], waw=False)
                    op("dve", lambda e: e.reciprocal(r_[:, 0, :], r_[:, 0, :]), [r_.b], [r_.b])
                    op("dve", lambda e: e.tensor_scalar(impf[:, :], IMP[:, 0, 0:NS], r_[:, 0, 0:1], None, op0=MUL),
                       [IMP.b, r_.b], [impf.b])
                    for r in (1, 2):
                        op("dve", lambda e, r=r: e.scalar_tensor_tensor(
                            impf[:, :], IMP[:, r, 0:NS], r_[:, 0, r:r + 1], impf[:, :], op0=MUL, op1=ADD),
                            [IMP.b, r_.b, impf.b], [impf.b])
                    lo = NS - 2 * qt
                    op("dve", lambda e: e.tensor_tensor(impf[:, :], impf[:, :], keepW[:, lo:lo + NS], MUL),
                       [impf.b, keepW.b], [impf.b])
                    op("dve", lambda e: e.tensor_tensor(impf[:, :], impf[:, :], addW[:, lo:lo + NS], ADD),
                       [impf.b, addW.b], [impf.b])
                    op("dve", lambda e: e.memset(impf[:, 0:1], 1e4), [], [impf.b])
                    op("dve", lambda e: e.max(m8[:, 0:8], impf[:, :]), [impf.b], [m8.b])
                    op("dve", lambda e: e.match_replace(imp2[:, :], m8[:, 0:8], impf[:, :], -1e9),
                       [m8.b, impf.b], [imp2.b])
                    op("dve", lambda e: e.max(m8[:, 8:16], imp2[:, :]), [imp2.b], [m8.b])
                    op("dve", lambda e: e.tensor_scalar(negq[:, 0:NS], impf[:, :], m8[:, 15:16], NEG,
                                                        op0=ALU.is_lt, op1=MUL),
                       [impf.b, m8.b], [negq.b])
                    h0 = min(NS, 64)
                    op("dve", lambda e: e.tensor_scalar(negq2[:, 64:64 + h0], impf[:, 0:h0], m8[:, 15:16], NEG,
                                                        op0=ALU.is_lt, op1=MUL),
                       [impf.b, m8.b], [negq2.b])
                    if NS > 64:
                        op("dve", lambda e: e.tensor_scalar(negq2[:, 0:64], impf[:, 64:128], m8[:, 15:16], NEG,
                                                            op0=ALU.is_lt, op1=MUL),
                           [impf.b, m8.b], [negq2.b], waw=False)
                    op("pe", lambda e: e.transpose(TP[:, 3, :], negq2[:, :], ident_b[:, :]),
                       [negq2.b, ident_b.b], [TP.b])
                    if NWIN > 1:
                        op("pe", lambda e: e.transpose(TP[:, 0, :], negq[:, :], ident_b[:, :]),
                           [negq.b, ident_b.b], [TP.b])
                    q0 = qt * 128
                    for w in range(NWIN):
                        if w * 32 >= qt:
                            continue
                        mq = MQ[w][n % 2]
                        op("pool", lambda e, mq=mq: e.tensor_copy(
                            mq[0:64, :].rearrange("p (r q) -> p r q", r=3), QN[0:64, :, q0:q0 + 128]),
                            [QN.b], [mq.b])
                        slot = 3 if w == 0 else 0
                        for r in range(3):
                            op("dve", lambda e, r=r, mq=mq, slot=slot: e.tensor_copy(
                                mq[64:128, r * 128:(r + 1) * 128], TP[64:128, slot, :]),
                                [TP.b], [mq.b], waw=False)
                return f

            def post_sel(n, qt, g):
                def f():
                    o = OCW[n % 2]
                    os_ = OS[n % 2]
                    r_ = rd[n % 2]
                    op("dve", lambda e: e.tensor_scalar(r_[:, 1, :], os_[:, :, 64], 1e-30, None, op0=ALU.max),
                       [os_.b], [r_.b], waw=False)
                    op("dve", lambda e: e.tensor_scalar(r_[:, 2, :], o[:, 1, :, 64], 1e-30, None, op0=ALU.max),
                       [o.b], [r_.b], waw=False)
                    op("dve", lambda e: e.reciprocal(r_[:, 1:3, :], r_[:, 1:3, :]), [r_.b], [r_.b])
                    gv = G_res[:, qt, g * 9:(g + 1) * 9].rearrange("p (r b) -> p b r", b=3)
                    op("dve", lambda e: e.tensor_tensor(coef[:, :, :], r_[:, :, :], gv, MUL),
                       [r_.b, G_res.b], [coef.b])
                    no = NSAO[qt % 2]
                    for r in range(3):
                        op("dve", lambda e, r=r: e.tensor_scalar(accf[:, :], o[:, 0, r, 0:64], coef[:, 0, r:r + 1],
                                                                  None, op0=MUL), [o.b, coef.b], [accf.b])
                        op("dve", lambda e, r=r: e.scalar_tensor_tensor(
                            accf[:, :], os_[:, r, 0:64], coef[:, 1, r:r + 1], accf[:, :], op0=MUL, op1=ADD),
                            [os_.b, coef.b, accf.b], [accf.b])
                        op("dve", lambda e, r=r: e.scalar_tensor_tensor(
                            no[:, g, r, :], o[:, 1, r, 0:64], coef[:, 2, r:r + 1], accf[:, :], op0=MUL, op1=ADD),
                            [o.b, coef.b, accf.b], [no.b], waw=(r == 0))
                    if True:
                        grp = qt // 4
                        sz = szt[grp % 2]
                        ys = yst[grp % 2]
                        if qt % 4 == 0:
                            t0 = grp * 512
                            dma("sp", sz[:, :, :], SZd.ap()[:, 0:2, t0:t0 + 512], SZd.b, sz.b)
                        nof = no[:, :, :, :].rearrange("p g r e -> p (g r e)")
                        for k in range(2):
                            op("pe", lambda e, k=k: e.transpose(TP[:, 1 + k, :], nof[:, k * 128:(k + 1) * 128],
                                                                 ident_b[:, :]),
                               [no.b, ident_b.b], [tpb[1 + k]])
                        for k in range(2):
                            qo = (qt % 4) * 128
                            op("dve", lambda e, k=k, qo=qo: e.tensor_tensor(
                                ys[:, k, qo:qo + 128], TP[:, 1 + k, :], sz[:, k, qo:qo + 128], MUL),
                                [tpb[1 + k], sz.b], [ys.b], waw=False)
                        if qt % 4 == 3 or qt == NT - 1:
                            t0 = grp * 512
                            dma("pool", fm(yTs[L][grp])[:, 1:3, :], ys[:, :, :], ys.b, yTs[L][grp].b, waw=False)
                return f

            def cmp_jobs(n, qt, g):
                q0 = qt * 128
                jobs = []
                cs = [c for c in range(NCC) if q0 - 2048 * c >= 0]
                for c in cs:
                    d = q0 - 2048 * c
                    bias = []
                    if d < 2176:
                        bias = [(r * 128, (r + 1) * 128, ident_b[:, :], cmpb[:, d // 128, :], [ident_b.b, cmpb.b])
                                for r in range(3)]
                    o = OCW[n % 2]
                    pv = []
                    for r in range(3):
                        pv.append((o[:, 0, r, :], r * 128, (r + 1) * 128, VCX[:, c, g, 0:65],
                                   c == cs[0], c == cs[-1], o.b, [VCX.b]))
                        pv.append((IMP[:, r, 0:NS], r * 128, (r + 1) * 128, VCX[:, c, g, 65:65 + NS],
                                   c == cs[0], c == cs[-1], IMP.b, [VCX.b]))
                    jobs.append(dict(ncol=384, bias=bias, pv=pv,
                                     qk=[(0, 384, KCMPT[g * 64:(g + 1) * 64, c * 128:(c + 1) * 128],
                                          q_ap(g, q0), [KCMPT.b, QN.b])]))
                jobs[-1]["post"] = post_cmp(n, qt, g)

                def pre(o=o):
                    op("dve", lambda e: e.memset(o[:, 0, :, :], 0.0), [], [o.b])
                    op("dve", lambda e: e.memset(IMP[:, :, :], 0.0), [], [IMP.b])
                jobs[0]["pre"] = pre
                return jobs

            def win_jobs(n, qt, g):
                q0 = qt * 128
                jobs = []
                cs = list(range(max(0, qt - 4), qt + 1))
                o = OCW[n % 2]
                for c in cs:
                    bias = []
                    if c == qt:
                        bias = [(0, 384, ident_b[:, :], causal3[:, :], [ident_b.b, causal3.b])]
                    elif c == qt - 4:
                        bias = [(0, 384, ident_b[:, :], winold3[:, :], [ident_b.b, winold3.b])]
                    pv = [(o[:, 1, r, :], r * 128, (r + 1) * 128, VSW[:, c, 1, 0:65], c == cs[0], c == cs[-1],
                           o.b, [VSW.b]) for r in range(3)]
                    jobs.append(dict(ncol=384, bias=bias, pv=pv,
                                     qk=[(0, 384, KSW[g * 64:(g + 1) * 64, 1, c * 128:(c + 1) * 128],
                                          q_ap(g, q0), [KSW.b, QN.b])]))

                def pre(o=o):
                    op("dve", lambda e: e.memset(o[:, 1, :, :], 0.0), [], [o.b])
                jobs[0]["pre"] = pre
                return jobs

            def sel_jobs(n, qt, g):
                q0 = qt * 128
                jobs = []
                o = OS[n % 2]
                for c in range(qt + 1):
                    pv = [(o[:, r, :], r * 128, (r + 1) * 128, VSW[:, c, g, 0:65], c == 0, c == qt,
                           o.b, [VSW.b]) for r in range(3)]
                    if c == qt:
                        bias = [(0, 384, ident_b[:, :], causal3[:, :], [ident_b.b, causal3.b])]
                        qk = [(0, 384, KSW[0:64, 0, c * 128:(c + 1) * 128], q_ap(g, q0), [KSW.b, QN.b])]
                    else:
                        bias = []
                        mq = MQ[c // 32][n % 2]
                        qk = [(0, 384, KSW[:, 0, c * 128:(c + 1) * 128], mq[:, :], [KSW.b, mq.b])]
                    jobs.append(dict(ncol=384, bias=bias, pv=pv, qk=qk))
                jobs[-1]["post"] = post_sel(n, qt, g)

                def pre(o=o):
                    op("dve", lambda e: e.memset(o[:, :, :], 0.0), [], [o.b])
                jobs[0]["pre"] = pre
                return jobs

            order = [(qt, 0) for qt in range(NT)]
            jobs = cmp_jobs(0, *order[0])
            for n, (qt, g) in enumerate(order):
                if n + 1 < len(order):
                    jobs += cmp_jobs(n + 1, *order[n + 1])
                jobs += win_jobs(n, qt, g)
                jobs += sel_jobs(n, qt, g)
            run_jobs(jobs, S, PT)
            cx.barrier(release=[QN.b, KSW.b, VSW.b, szt[0].b, szt[1].b])

    DIL = ((128, 1), (512, 4), (2048, 16))

    def phaseD(L, do_exchange, between=None):
        PAIRLOC = ((0, 0), (0, 1), (1, 0))
        with ExitStack() as ps:
            DQK = TL(cx, ps, "DQK", [128, 4, T], BF16)
            for r in range(4):
                dma("sp", DQK[:, r, :], DQKd.ap()[:, r, :], DQKd.b, DQK.b, waw=False)
            DVp = [TL(cx, ps, "DVp%d" % i, [128, NT, 1, 66], BF16) for i in range(3)]
            PT = [TL(cx, ps, "PTd%d" % i, [128, 512], BF16) for i in range(4)]
            ost = [TL(cx, ps, "ost%d" % i, [128, 4, 65], F32) for i in range(3)]
            gcnt = [0]
            S = [TL(cx, ps, "Sd%d" % i, [128, 2, 512], F32, psum=True) for i in range(3)]
            OD = [TL(cx, ps, "OD%d" % i, [128, 1, 65], F32, psum=True) for i in range(2)]
            cnt = [0]
            jobs = []
            for i, (W_, d) in enumerate(DIL):
                Lq = T // d
                nbk = Lq // 128
                if nbk == 0:
                    raise ValueError("sequence too short for dilation")
                cq, hf = PAIRLOC[i]
                hp = slice(hf * 64, (hf + 1) * 64)
                dv = DVp[i]
                for rho in range(d):
                    src = DVd.ap()[rho:T:d, i:i + 1, :].rearrange("(n p) h e -> p n h e", p=128)
                    nsp = max(1, nbk // 16)
                    for k in range(nsp):
                        a, b = k * nbk // nsp, (k + 1) * nbk // nsp
                        dma("sp", dv[:, rho * nbk + a:rho * nbk + b, :, :], src[:, a:b, :, :], DVd.b, dv.b,
                            waw=(rho == 0 and k == 0))
                for rho in range(d):
                    for nb in range(nbk):
                        qs = nb * 128 * d + rho
                        ksl = {1: slice(qs, qs + 127 * d + 1, d)}
                        if nb > 0:
                            ks0 = (nb - 1) * 128 * d + rho
                            ksl[0] = slice(ks0, ks0 + 127 * d + 1, d)
                        n = cnt[0]
                        cnt[0] += 1
                        o = OD[n % 2]
                        pcs = sorted(ksl.keys())
                        qk = [(hf, pc * 128, (pc + 1) * 128, DQK[hp, 2 + cq, ksl[pc]], DQK[hp, cq, ksl[1]], [DQK.b])
                              for pc in pcs]
                        pv = [(o[:, 0, :], pc * 128, (pc + 1) * 128, dv[:, rho * nbk + (nb - 1 + pc), 0, 0:65],
                               pc == pcs[0], pc == pcs[-1], o.b, [dv.b]) for pc in pcs]

                        G = min(4, nbk)
                        if nb % G == 0:
                            gcnt[0] += 1
                        gi = gcnt[0]

                        def post(o=o, i=i, d=d, nb=nb, rho=rho, G=G, gi=gi):
                            st_ = ost[gi % 3]
                            op("dve", lambda e: e.tensor_copy(st_[:, nb % G, :], o[:, 0, :]), [o.b], [st_.b],
                               waw=False)
                            if nb % G == G - 1:
                                r0 = (nb - (G - 1)) * 128 * d + rho
                                dst = DOd.ap()[r0:r0 + (G * 128 - 1) * d + 1:d, i:i + 1, :].rearrange(
                                    "(n p) i e -> p n (i e)", p=128)
                                dma("pool", dst, st_[:, 0:G, :], st_.b, DOd.b, waw=False)

                        def pre(o=o):
                            op("dve", lambda e: e.memset(o[:, :, :], 0.0), [], [o.b])
                        jobs.append(dict(qk=qk, pv=pv, post=post, pre=pre, hf=hf))
            run_jobs_d(jobs, S, PT)
            cx.barrier(release=[DQK.b, DVp[0].b, DVp[1].b, DVp[2].b])

        if between is not None:
            between()
        with ExitStack() as ps:
            dot = [TL(cx, ps, "dot%d" % i, [128, 3, 65], F32) for i in range(2)]
            dsum = TL(cx, ps, "dsum", [128, 1], F32)
            yc = TL(cx, ps, "yc", [128, 4, 64], BF16)
            szt = [TL(cx, ps, "sztd%d" % i, [128, 2, 512], BF16) for i in range(2)]
            yst = [TL(cx, ps, "ystd%d" % i, [128, 2, 512], BF16) for i in range(2)]
            TP = TL(cx, ps, "TPd", [128, 4, 128], BF16, psum=True)
            op("pool", lambda e: e.memset(yc[:, :, :], 0.0), [], [yc.b])
            for c in range(NT):
                do = dot[c % 2]
                dma("sp", do[:, :, :], DOd.ap()[c * 128:(c + 1) * 128, :, :], DOd.b, do.b)
                grp = c // 4
                sz = szt[grp % 2]
                ys = yst[grp % 2]
                if c % 4 == 0:
                    dma("sp", sz[:, :, :], SZd.ap()[:, 2:4, grp * 512:(grp + 1) * 512], SZd.b, sz.b)
                op("dve", lambda e, do=do: e.tensor_tensor(dsum[:, :], do[:, 0, 64:65], do[:, 1, 64:65], ADD),
                   [do.b], [dsum.b])
                op("dve", lambda e, do=do: e.tensor_tensor(dsum[:, :], dsum[:, :], do[:, 2, 64:65], ADD),
                   [do.b, dsum.b], [dsum.b])
                op("dve", lambda e: e.reciprocal(dsum[:, :], dsum[:, :]), [dsum.b], [dsum.b])
                op("dve", lambda e, do=do: e.tensor_scalar(
                    yc[:, 0:3, :], do[:, :, 0:64], dsum[:, 0:1], None, op0=MUL),
                    [do.b, dsum.b], [yc.b])
                ycf = yc[:, :, :].rearrange("p i e -> p (i e)")
                qo = (c % 4) * 128
                for k in range(2):
                    op("pe", lambda e, k=k: e.transpose(TP[:, k, :], ycf[:, k * 128:(k + 1) * 128], ident_b[:, :]),
                       [yc.b, ident_b.b], [TP.b])
                for k in range(2):
                    op("dve", lambda e, k=k, qo=qo, ys=ys, sz=sz: e.tensor_tensor(
                        ys[:, k, qo:qo + 128], TP[:, k, :], sz[:, k, qo:qo + 128], MUL),
                        [TP.b, sz.b], [ys.b], waw=False)
                if c % 4 == 3:
                    dma("pool", fm(yTs[L][grp])[:, 3:5, :], ys[:, :, :], ys.b, yTs[L][grp].b, waw=False)
                    if do_exchange:
                        exchange(L, grp)
            cx.barrier(release=[dot[0].b, dot[1].b, szt[0].b, szt[1].b])

    def exchange(L, tt):
        e = cx.engs["pool"]
        src, dst = yTs[L][tt], yTfs[L][tt]
        cx._sync(e, [src.b], [dst.b], True)
        ins = nc.gpsimd.collective_compute(
            "AllGather", ALU.bypass, replica_groups=[[2 * b_, 2 * b_ + 1] for b_ in range(n_pairs)],
            ins=[src.t.ap().opt()], outs=[dst.t.ap().opt()])
        ccn[0] += 1
        ins.then_inc(ccsem, 1)
        tok = ("cc", ccsem, ccn[0])
        dst.b.w["cc"] = tok
        src.b.r["cc"] = tok
        cx.extra_toks["cc"] = tok

    def prepE(L, stack):
        NK = 2 * NYC
        wo = TL(cx, stack, "wo", [128, NK, D_MODEL], BF16)
        wg = TL(cx, stack, "wg", [128, 8, D_MODEL], BF16)
        wp = TL(cx, stack, "wp", [128, 2, D_MODEL], BF16)
        wst = [TL(cx, stack, "wste%d" % i, [128, D_MODEL], F32) for i in range(2)]
        k = 0
        for (src, dst, nk) in ((wout, wo, NK), (wgate, wg, 8), (wproj, wp, 2)):
            for kc in range(nk):
                s = wst[k % 2]
                k += 1
                dma("sp", s[:, :], src.ap()[L, :, kc, :], src.b, s.b)
                op("act", lambda e, dst=dst, kc=kc, s=s: e.copy(dst[:, kc, :], s[:, :]),
                   [s.b], [dst.b], waw=False)
        return wo, wg, wp, wst

    def phaseE(L, last, W):
        src_h = xT if L == 0 else hT
        NK = 2 * NYC
        wo, wg, wp, wst = W
        with ExitStack() as ps:
            yt = [TL(cx, ps, "yt%d" % i, [128, NK, 512], BF16) for i in range(2)]
            htl = [TL(cx, ps, "htl%d" % i, [128, 8, 512], F32) for i in range(2)]
            pts = [TL(cx, ps, "pts%d" % i, [128, 2, 512], F32) for i in range(2)]
            ptb = TL(cx, ps, "ptb", [128, 2, 512], BF16)
            h1b = TL(cx, ps, "h1b", [128, 8, 512], BF16)
            sig = [TL(cx, ps, "sig%d" % i, [128, 512], F32) for i in range(2)]
            sqf = [TL(cx, ps, "sqf%d" % i, [128, 512], F32) for i in range(2)]
            rbc = TL(cx, ps, "rbcE", [128, 512], F32)
            po = [TL(cx, ps, "po%d" % i, [128, 512], F32, psum=True) for i in range(2)]
            pg = [TL(cx, ps, "pg%d" % i, [128, 512], F32, psum=True) for i in range(2)]
            pp = [TL(cx, ps, "pp%d" % i, [128, 512], F32, psum=True) for i in range(2)]
            pst = TL(cx, ps, "pstE", [128, 512], F32, psum=True)
            dst_h = outT if last else hT

            def loads(tt):
                t0 = tt * 512
                dma("sp", yt[tt % 2][:, :, :], fm(yTfs[L][tt])[:, :, :], yTfs[L][tt].b, yt[tt % 2].b)
                dma("sp", htl[tt % 2][:, :, :], fm(src_h)[:, :, t0:t0 + 512], src_h.b, htl[tt % 2].b)
                dma("sp", pts[tt % 2][:, :, :],
                    pT.ap()[L, :, t0:t0 + 512].rearrange("(c p) t -> p c t", p=128), pT.b, pts[tt % 2].b)

            loads(0)
            for tt in range(NTT):
                t0 = tt * 512
                if tt + 1 < NTT:
                    loads(tt + 1)
                y_, h_, p_ = yt[tt % 2], htl[tt % 2], pts[tt % 2]
                op("pool", lambda e, p_=p_: e.tensor_copy(ptb[:, :, :], p_[:, :, :]), [p_.b], [ptb.b])
                for j in range(8):
                    p = po[j % 2]
                    for kc in range(NK):
                        mm(p[:, :], wo[:, kc, j * 128:(j + 1) * 128], y_[:, kc, :], kc == 0, kc == NK - 1,
                           [wo.b, y_.b], [p.b])
                    op("dve", lambda e, j=j, p=p, h_=h_: e.tensor_tensor(h_[:, j, :], p[:, :], h_[:, j, :], ADD),
                       [p.b, h_.b], [h_.b])
                    op("act", lambda e, j=j, h_=h_: e.copy(h1b[:, j, :], h_[:, j, :]), [h_.b], [h1b.b],
                       waw=False)
                for j in range(8):
                    g_ = pg[j % 2]
                    q_ = pp[j % 2]
                    sg = sig[j % 2]
                    for kc in range(8):
                        mm(g_[:, :], wg[:, kc, j * 128:(j + 1) * 128], h1b[:, kc, :], kc == 0, kc == 7,
                           [wg.b, h1b.b], [g_.b])
                    for kc in range(2):
                        mm(q_[:, :], wp[:, kc, j * 128:(j + 1) * 128], ptb[:, kc, :], kc == 0, kc == 1,
                           [wp.b, ptb.b], [q_.b])
                    op("act", lambda e, sg=sg, g_=g_: e.activation(sg[:, :], g_[:, :], AF.Sigmoid), [g_.b], [sg.b])
                    op("dve", lambda e, sg=sg, q_=q_: e.tensor_tensor(sg[:, :], sg[:, :], q_[:, :], MUL),
                       [sg.b, q_.b], [sg.b])
                    op("pool", lambda e, sg=sg, j=j, h_=h_: e.tensor_tensor(h_[:, j, :], h_[:, j, :], sg[:, :], ADD),
                       [sg.b, h_.b], [h_.b])
                if last:
                    for j in range(8):
                        sq = sqf[j % 2]
                        op("act", lambda e, sq=sq, j=j, h_=h_: e.activation(sq[:, :], h_[:, j, :], AF.Square),
                           [h_.b], [sq.b])
                        mm(pst[:, :], ones_f[:, :], sq[:, :], j == 0, j == 7, [ones_f.b, sq.b], [pst.b])
                    op("dve", lambda e: e.tensor_scalar(rbc[:, :], pst[:, :], 1.0 / D_MODEL, EPS, op0=MUL, op1=ADD),
                       [pst.b], [rbc.b])
                    op("act", lambda e: e.activation(rbc[:, :], rbc[:, :], AF.Sqrt), [rbc.b], [rbc.b])
                    op("dve", lambda e: e.reciprocal(rbc[:, :], rbc[:, :]), [rbc.b], [rbc.b])
                    for j in range(8):
                        op("dve", lambda e, j=j, h_=h_: e.scalar_tensor_tensor(
                            h_[:, j, :], h_[:, j, :], gfin_s[:, j:j + 1], rbc[:, :], op0=MUL, op1=MUL),
                            [h_.b, gfin_s.b, rbc.b], [h_.b])
                dma("pool", fm(dst_h)[:, :, t0:t0 + 512], h_[:, :, :], h_.b, dst_h.b, waw=False)
            rel = [b.b for b in yt + htl + pts + wst]
            cx.barrier(release=rel)

    for L in range(depth if nlayers is None else nlayers):
        with ExitStack() as ls:
            G_res = TL(cx, ls, "G_res", [128, NT, 10], F32)
            KCMPT = TL(cx, ls, "KCMPT", [128, NCC * 128], BF16)
            VCX = TL(cx, ls, "VCX", [128, NCC, 2, 65 + NS], BF16)
            with ExitStack() as ab:
                KVC = TL(cx, ab, "KVC", [128, 2, T], BF16)
                if "A" in phases:
                    phaseA(L, G_res, KVC)
                if "B" in phases:
                    phaseB(L, KVC, KCMPT, VCX)
            if "C" in phases:
                phaseC(L, G_res, KCMPT, VCX)
            with ExitStack() as ee:
                W = []
                if "D" in phases:
                    phaseD(L, "X" in phases, (lambda: W.append(prepE(L, ee))) if "E" in phases else None)
                if "E" in phases:
                    if not W:
                        W.append(prepE(L, ee))
                    phaseE(L, L == depth - 1, W[0])
    cx.barrier()
    es.close()
    return nc


def _consts(T):
    NS = T // 64
    NCC = max(1, T // 2048)
    NCMP = T // 16 - 1
    j = np.arange(128)[:, None]
    i = np.arange(128)[None, :]
    f = np.float32
    c = {}
    c["c_ident"] = np.eye(128, dtype=f)
    c["c_causal"] = np.where(j <= i, 0.0, NEG).astype(f)
    c["c_winold"] = np.where(j > i, 0.0, NEG).astype(f)
    c["c_dprev"] = np.where(j >= i, 0.0, NEG).astype(f)
    cm = np.zeros((128, 17, 128), f)
    for di in range(17):
        cm[:, di, :] = np.where(16 * j + 31 <= di * 128 + i, 0.0, NEG)
    c["c_cmpb"] = cm
    we = np.zeros((128, T), f)
    m = np.arange(T)
    we[64 + (m // 64) % 64, m] = 1.0
    c["c_wexp"] = we
    ov = np.zeros((128, NCC, NS), f)
    for cc in range(NCC):
        n = cc * 128 + np.arange(128)[:, None]
        s = np.arange(NS)[None, :]
        o = (16 * n < 64 * s + 64) & (16 * n + 32 > 64 * s) & (n < NCMP)
        ov[:, cc, :] = o
    c["c_ovl"] = ov
    keep = np.zeros((128, 2 * NS), f)
    add = np.zeros((128, 2 * NS), f)
    ii = np.arange(128)
    for xi in range(2 * NS):
        x = xi - NS
        if x <= -2:
            keep[:, xi] = 1.0
        elif x == -1:
            keep[:, xi] = (ii >= 64)
            add[:, xi] = np.where(ii < 64, 1e4, 0.0)
        elif x == 0:
            add[:, xi] = 1e4
        elif x == 1:
            add[:, xi] = np.where(ii >= 64, 1e4, -1.0)
        else:
            add[:, xi] = -1.0
    c["c_keep"] = keep
    c["c_add"] = add
    return c


ZC = 4114


def _col_perm(s_):
    z64 = [ZC] * 64
    r64 = lambda o: list(range(o, o + 64))
    fmc = []
    for off in (0, 256, 512, 768):
        fmc += list(range(off + s_ * 128, off + s_ * 128 + 128))
    for r in range(3):
        fmc += r64(1024 + (s_ * 3 + r) * 64) * 2
    for off in (1408, 1536, 1664, 1920):
        fmc += r64(off + s_ * 64) * 2
    fmc += list(range(2194 + s_ * 192, 2194 + s_ * 192 + 192)) + z64
    heads = [s_, 2 + s_, 4 + s_]
    for off in (2578, 2962, 3730):
        fmc += r64(off + heads[0] * 64) + r64(off + heads[1] * 64) + r64(off + heads[2] * 64) + z64
    tmc = r64(1792 + s_ * 64) + r64(2048 + s_ * 64)
    for h in heads:
        tmc += r64(3346 + h * 64)
    tmc += list(range(2176 + s_ * 9, 2176 + s_ * 9 + 9)) + [ZC]
    assert len(fmc) == NFM * 128 and len(tmc) == NTM, (len(fmc), len(tmc))
    return np.array(fmc + tmc)


def _weights(s_, norm_mix, w_in, conv_w, conv_b, cmp_pe, cmp_w1, cmp_w2, w_out, w_ple_gate, w_ple_proj,
             norm_final):
    depth = w_in.shape[0]
    f = np.float32
    perm = _col_perm(s_)
    d = {}
    wz = np.concatenate([np.asarray(w_in, f), np.zeros((depth, D_MODEL, 1), f)], axis=2)
    d["win"] = np.ascontiguousarray(wz[:, :, perm].reshape(depth, 8, 128, D_IN).transpose(0, 2, 1, 3))
    d["gin"] = np.ascontiguousarray(np.asarray(norm_mix, f).reshape(depth, 8, 128).transpose(0, 2, 1))
    cw = np.zeros((depth, 128, 1, 4), f)
    cw[:, :, 0, 0:3] = np.asarray(conv_w, f).reshape(depth, 3, 2, 128)[:, :, s_, :].transpose(0, 2, 1)
    cw[:, :, 0, 3] = np.asarray(conv_b, f).reshape(depth, 2, 128)[:, s_, :]
    d["convw"] = cw
    w1 = np.asarray(cmp_w1, f).reshape(depth, 2, 32, 64, 64)
    w1b = np.zeros((depth, 2, 2, 64, 32, 2, 64), f)
    for g in range(2):
        w1b[:, :, g, :, :, g, :] = w1.transpose(0, 1, 3, 2, 4)
    d["w1blk"] = w1b.reshape(depth, 2, 128, 32, 128)
    w2 = np.asarray(cmp_w2, f)
    w2b = np.zeros((depth, 2, 2, 64, 2, 64), f)
    for g in range(2):
        w2b[:, :, g, :, g, :] = w2
    d["w2blk"] = w2b.reshape(depth, 2, 128, 128)
    pe = np.asarray(cmp_pe, f).transpose(0, 1, 3, 2)
    d["peT"] = np.ascontiguousarray(np.concatenate([pe, pe], axis=2))
    wo = np.concatenate([np.asarray(w_out, f), np.zeros((depth, 1, D_MODEL), f)], axis=1)
    ZR = D_MODEL
    rows = []
    for q in range(2):
        rows += list(range(q * 128, q * 128 + 128))
        rows += list(range(256 + q * 192, 256 + q * 192 + 192)) + [ZR] * 64
        for h in (q, 2 + q, 4 + q):
            rows += list(range(640 + h * 64, 640 + h * 64 + 64))
        rows += [ZR] * 64
    assert len(rows) == 2 * NYC * 128
    d["wout"] = np.ascontiguousarray(wo[:, rows, :].reshape(depth, 2 * NYC, 128, D_MODEL).transpose(0, 2, 1, 3))
    d["wgate"] = np.ascontiguousarray(np.asarray(w_ple_gate, f).reshape(depth, 8, 128, D_MODEL).transpose(0, 2, 1, 3))
    d["wproj"] = np.ascontiguousarray(np.asarray(w_ple_proj, f).reshape(depth, 2, 128, D_MODEL).transpose(0, 2, 1, 3))
    d["gfin"] = np.ascontiguousarray(np.asarray(norm_final, f).reshape(8, 128).T)
    return d


_NC_CACHE = {}


def run(x, p, weights, debug=False, **bk):
    x = np.asarray(x, np.float32)
    p = np.asarray(p, np.float32)
    B, T, _ = x.shape
    depth = p.shape[0]
    key = (T, depth, debug, tuple(sorted(bk.items())))
    if key not in _NC_CACHE:
        _NC_CACHE[key] = build(T, depth, debug, n_pairs=B, **bk)
    nc = _NC_CACHE[key]
    consts = _consts(T)
    wts = [_weights(s_, **weights) for s_ in range(2)]
    in_maps = []
    for b in range(B):
        xT = np.ascontiguousarray(x[b].T)
        pTb = np.ascontiguousarray(p[:, b].transpose(0, 2, 1))
        for s_ in range(2):
            m = dict(wts[s_])
            m.update(consts)
            m["xT"] = xT
            m["pT"] = pTb
            in_maps.append(m)
    res = run_bass_kernel_spmd(nc, in_maps, core_ids=list(range(2 * B)))
    out = np.stack([np.ascontiguousarray(np.asarray(res.results[2 * b]["outT"]).T) for b in range(B)], axis=0)
    return out.astype(np.float32), res


def kernel(x, p, norm_mix, w_in, conv_w, conv_b, cmp_pe, cmp_w1, cmp_w2, w_out, w_ple_gate, w_ple_proj,
           norm_final):
    weights = dict(norm_mix=norm_mix, w_in=w_in, conv_w=conv_w, conv_b=conv_b, cmp_pe=cmp_pe, cmp_w1=cmp_w1,
                   cmp_w2=cmp_w2, w_out=w_out, w_ple_gate=w_ple_gate, w_ple_proj=w_ple_proj,
                   norm_final=norm_final)
    out, _ = run(x, p, weights)
    return out
```
